# Optimizing a Trainium2 kernel written in Bass

```python
import math
import jax, jax.numpy as jnp
from jax import lax
import numpy as np

D_MODEL = 1024
BATCH = 4
SEQ = 4096
DEPTH = 1

N_META = 16
HEAD_DIM = 64
SWA_Q_HEADS = D_MODEL // (2 * HEAD_DIM)
SWA_KV_HEADS = max(SWA_Q_HEADS // 4, 1)
SWA_GROUP = SWA_Q_HEADS // SWA_KV_HEADS
FOX_HEADS = D_MODEL // (2 * HEAD_DIM)
SWA_Q_W = SWA_Q_HEADS * HEAD_DIM
SWA_KV_W = SWA_KV_HEADS * HEAD_DIM
FOX_W = FOX_HEADS * HEAD_DIM
D_MIX = SWA_Q_W + FOX_W
OFF_QA = SWA_Q_W
OFF_KA = OFF_QA + SWA_KV_W
OFF_VA = OFF_KA + SWA_KV_W
OFF_QB = OFF_VA + FOX_W
OFF_KB = OFF_QB + FOX_W
OFF_VB = OFF_KB + FOX_W
D_PROJ = OFF_VB + FOX_HEADS
WINDOW = 128
BLOCK = 128
N_BUCKETS = 32
MAX_DISTANCE = 128
D_FF = -(-8 * D_MODEL // (3 * 256)) * 256
EPS = 1e-6
NEG_INF = -1e30

kernel_name = "hymba_swa_sink_fox_t5bias_sandwich"


def rms_norm(x, g):
    xf = x.astype(jnp.float32)
    y = xf * lax.rsqrt(jnp.mean(xf * xf, axis=-1, keepdims=True) + EPS)
    return (y * g.astype(jnp.float32)).astype(x.dtype)


def t5_bucket(dist):
    n = jnp.maximum(dist, 0).astype(jnp.int32)
    max_exact = N_BUCKETS // 2
    nf = jnp.maximum(n, 1).astype(jnp.float32)
    large = max_exact + (jnp.log(nf / max_exact) / math.log(MAX_DISTANCE / max_exact)
                         * (N_BUCKETS - max_exact)).astype(jnp.int32)
    large = jnp.minimum(large, N_BUCKETS - 1)
    return jnp.where(n < max_exact, n, large)


def softmax_with_sink(s, sink):
    sink_r = sink.reshape((1,) + sink.shape + (1,) * (s.ndim - 3))
    col = jnp.broadcast_to(sink_r, s.shape[:-1] + (1,))
    p = jax.nn.softmax(jnp.concatenate([s, col], axis=-1), axis=-1)
    return p[..., :-1]


def swa_sink_attention(q, k, v, sinks, rel_bias):
    B, L = q.shape[0], q.shape[1]
    n_blk = (L - N_META) // BLOCK
    scale = HEAD_DIM ** -0.5
    sink = sinks.astype(jnp.float32).reshape(SWA_KV_HEADS, SWA_GROUP)
    tab = rel_bias.astype(jnp.float32)
    mi = jnp.arange(N_META)
    km, vm = k[:, :N_META], v[:, :N_META]

    qm = q[:, :N_META].reshape(B, N_META, SWA_KV_HEADS, SWA_GROUP, HEAD_DIM)
    d_mm = mi[:, None] - mi[None, :]
    b_mm = tab[t5_bucket(d_mm)].transpose(2, 0, 1).reshape(SWA_KV_HEADS, SWA_GROUP, N_META, N_META)
    s_mm = jnp.einsum('bqhgd,bkhd->bhgqk', qm, km, preferred_element_type=jnp.float32) * scale + b_mm
    s_mm = jnp.where(d_mm >= 0, s_mm, NEG_INF)
    p_mm = softmax_with_sink(s_mm, sink).astype(v.dtype)
    o_meta = jnp.einsum('bhgqk,bkhd->bqhgd', p_mm, vm).reshape(B, N_META, SWA_Q_HEADS, HEAD_DIM)

    qr = q[:, N_META:].reshape(B, n_blk, BLOCK, SWA_KV_HEADS, SWA_GROUP, HEAD_DIM)
    kr = k[:, N_META:].reshape(B, n_blk, BLOCK, SWA_KV_HEADS, HEAD_DIM)
    vr = v[:, N_META:].reshape(B, n_blk, BLOCK, SWA_KV_HEADS, HEAD_DIM)

    def with_prev(t):
        prev = jnp.pad(t, ((0, 0), (1, 0), (0, 0), (0, 0), (0, 0)))[:, :-1]
        return jnp.concatenate([prev, t], axis=2)

    kw, vw = with_prev(kr), with_prev(vr)
    qi = jnp.arange(BLOCK)[:, None]
    ki = jnp.arange(2 * BLOCK)[None, :]
    d_w = qi + BLOCK - ki
    b_w = tab[t5_bucket(d_w)].transpose(2, 0, 1).reshape(SWA_KV_HEADS, SWA_GROUP, 1, BLOCK, 2 * BLOCK)
    blk = jnp.arange(n_blk)[:, None, None]
    valid_w = ((d_w >= 0) & (d_w < WINDOW))[None] & ((blk > 0) | (ki >= BLOCK)[None])
    s_w = jnp.einsum('bnqhgd,bnkhd->bhgnqk', qr, kw, preferred_element_type=jnp.float32) * scale + b_w
    s_w = jnp.where(valid_w, s_w, NEG_INF)
    d_m = N_META + blk * BLOCK + qi[None] - mi[None, None, :]
    b_m = tab[t5_bucket(d_m)].transpose(3, 0, 1, 2).reshape(SWA_KV_HEADS, SWA_GROUP, n_blk, BLOCK, N_META)
    s_m = jnp.einsum('bnqhgd,bmhd->bhgnqm', qr, km, preferred_element_type=jnp.float32) * scale + b_m
    p = softmax_with_sink(jnp.concatenate([s_m, s_w], axis=-1), sink).astype(v.dtype)
    o = (jnp.einsum('bhgnqm,bmhd->bnqhgd', p[..., :N_META], vm)
         + jnp.einsum('bhgnqk,bnkhd->bnqhgd', p[..., N_META:], vw))
    o_real = o.reshape(B, n_blk * BLOCK, SWA_Q_HEADS, HEAD_DIM)
    return jnp.concatenate([o_meta, o_real], axis=1)


def forgetting_attention(q, k, v, f_logit):
    B, L = q.shape[0], q.shape[1]
    n_blk = (L - N_META) // BLOCK
    scale = HEAD_DIM ** -0.5
    cum = jnp.cumsum(jax.nn.log_sigmoid(f_logit.astype(jnp.float32)), axis=1).transpose(0, 2, 1)

    mi = jnp.arange(N_META)
    s = jnp.einsum('bqhd,bkhd->bhqk', q[:, :N_META], k[:, :N_META], preferred_element_type=jnp.float32) * scale
    s = s + cum[:, :, :N_META, None] - cum[:, :, None, :N_META]
    s = jnp.where(mi[:, None] >= mi[None, :], s, NEG_INF)
    p = jax.nn.softmax(s, axis=-1).astype(v.dtype)
    o_meta = jnp.einsum('bhqk,bkhd->bqhd', p, v[:, :N_META])

    qb = q[:, N_META:].reshape(B, n_blk, BLOCK, FOX_HEADS, HEAD_DIM).transpose(1, 0, 2, 3, 4)
    cb = cum[:, :, N_META:].reshape(B, FOX_HEADS, n_blk, BLOCK).transpose(2, 0, 1, 3)
    k_pos = jnp.arange(L)

    def one_block(args):
        q_blk, c_blk, b = args
        q_pos = N_META + b * BLOCK + jnp.arange(BLOCK)
        sb = jnp.einsum('bqhd,bkhd->bhqk', q_blk, k, preferred_element_type=jnp.float32) * scale
        sb = sb + c_blk[..., None] - cum[:, :, None, :]
        sb = jnp.where(k_pos[None, :] <= q_pos[:, None], sb, NEG_INF)
        pb = jax.nn.softmax(sb, axis=-1).astype(v.dtype)
        return jnp.einsum('bhqk,bkhd->bqhd', pb, v)

    o = lax.map(one_block, (qb, cb, jnp.arange(n_blk)))
    o_real = o.transpose(1, 0, 2, 3, 4).reshape(B, n_blk * BLOCK, FOX_HEADS, HEAD_DIM)
    return jnp.concatenate([o_meta, o_real], axis=1)


def setup_inputs(seed: int = 0) -> dict:
    key = jax.random.key(seed)
    ks = jax.random.split(key, 14)
    f32 = jnp.float32
    gain = lambda k: 1.0 + 0.05 * jax.random.normal(k, (DEPTH, D_MODEL), f32)
    return {
        "x": jax.random.normal(ks[0], (BATCH, SEQ, D_MODEL), f32),
        "meta_tokens": jax.random.normal(ks[1], (N_META, D_MODEL), f32),
        "rel_bias": 0.5 * jax.random.normal(ks[2], (N_BUCKETS, SWA_Q_HEADS), f32),
        "ln_pre_mix": gain(ks[3]),
        "ln_post_mix": gain(ks[4]),
        "ln_pre_ffn": gain(ks[5]),
        "ln_post_ffn": gain(ks[6]),
        "w_in": jax.random.normal(ks[7], (DEPTH, D_MODEL, D_PROJ), f32) * D_MODEL ** -0.5,
        "b_forget": jax.random.uniform(ks[8], (DEPTH, FOX_HEADS), f32, minval=1.0, maxval=5.0),
        "sinks": 0.5 * jax.random.normal(ks[9], (DEPTH, SWA_Q_HEADS), f32),
        "w_out": jax.random.normal(ks[10], (DEPTH, D_MIX, D_MODEL), f32) * D_MIX ** -0.5,
        "w_gate_up": jax.random.normal(ks[11], (DEPTH, D_MODEL, 2 * D_FF), f32) * D_MODEL ** -0.5,
        "w_down": jax.random.normal(ks[12], (DEPTH, D_FF, D_MODEL), f32) * D_FF ** -0.5,
    }


def reference(x, meta_tokens, rel_bias, ln_pre_mix, ln_post_mix, ln_pre_ffn, ln_post_ffn,
              w_in, b_forget, sinks, w_out, w_gate_up, w_down):
    B = x.shape[0]
    meta = jnp.broadcast_to(meta_tokens[None].astype(x.dtype), (B, N_META, D_MODEL))
    h = jnp.concatenate([meta, x], axis=1)
    L = h.shape[1]
    for layer in range(DEPTH):
        hn = rms_norm(h, ln_pre_mix[layer])
        proj = jnp.einsum('bld,dc->blc', hn, w_in[layer])
        q_a = proj[..., :OFF_QA].reshape(B, L, SWA_Q_HEADS, HEAD_DIM)
        k_a = proj[..., OFF_QA:OFF_KA].reshape(B, L, SWA_KV_HEADS, HEAD_DIM)
        v_a = proj[..., OFF_KA:OFF_VA].reshape(B, L, SWA_KV_HEADS, HEAD_DIM)
        q_b = proj[..., OFF_VA:OFF_QB].reshape(B, L, FOX_HEADS, HEAD_DIM)
        k_b = proj[..., OFF_QB:OFF_KB].reshape(B, L, FOX_HEADS, HEAD_DIM)
        v_b = proj[..., OFF_KB:OFF_VB].reshape(B, L, FOX_HEADS, HEAD_DIM)
        f_b = proj[..., OFF_VB:] + b_forget[layer].astype(proj.dtype)
        o_a = swa_sink_attention(q_a, k_a, v_a, sinks[layer], rel_bias)
        o_b = forgetting_attention(q_b, k_b, v_b, f_b)
        mix = jnp.concatenate([o_a.reshape(B, L, SWA_Q_W), o_b.reshape(B, L, FOX_W)], axis=-1)
        h = h + rms_norm(jnp.einsum('blc,cd->bld', mix, w_out[layer]), ln_post_mix[layer])
        hn = rms_norm(h, ln_pre_ffn[layer])
        gu = jnp.einsum('bld,df->blf', hn, w_gate_up[layer])
        ff = jnp.einsum('blf,fd->bld', jax.nn.silu(gu[..., :D_FF]) * gu[..., D_FF:], w_down[layer])
        h = h + rms_norm(ff, ln_post_ffn[layer])
    return h[:, N_META:]
```

```python
import contextlib
import numpy as np
import concourse.bass as bass
import concourse.mybir as mybir
from concourse.bass_utils import run_bass_kernel_spmd

F32 = mybir.dt.float32
BF16 = mybir.dt.bfloat16
AF = mybir.ActivationFunctionType
ALU = mybir.AluOpType

D = 1024
NT = 33
NS = 16
DFF = 2816
NF = 22
EPS = 1e-6
NEG = -30000.0

C_QA, C_KA, C_VA, C_F, C_QB, C_KB, C_VB = 0, 512, 640, 768, 776, 1288, 1800


class Op:
    __slots__ = ("eng", "fn", "deps", "lane", "token", "needed", "idx", "is_dma")

    def __init__(self, eng, fn, deps, lane=None):
        self.eng = eng
        self.fn = fn
        self.deps = deps
        self.lane = lane
        self.token = None
        self.needed = False
        self.is_dma = lane is not None


class Prog:
    ENGS = ("pe", "act", "dve", "pool", "sp")

    def __init__(self):
        self.ops = {e: [] for e in self.ENGS}
        self.last_writer = {}
        self.readers = {}
        self.barrier_ops = []
        self.lanes = {}
        self.n = 0
        self.limit = None

    @staticmethod
    def _sig(op):
        return op.lane if op.is_dma else op.eng

    def op(self, eng, fn, reads=(), writes=(), lane=None, extra=(), force=False):
        if self.limit is not None and self.n >= self.limit and not force:
            return None
        deps = {}

        def add(d):
            if d is None:
                return
            k = self._sig(d)
            cur = deps.get(k)
            if cur is None or cur.idx < d.idx:
                deps[k] = d

        for k in reads:
            add(self.last_writer.get(k))
        for k in writes:
            add(self.last_writer.get(k))
            for r in self.readers.get(k, {}).values():
                add(r)
        for d in extra:
            add(d)
        for d in self.barrier_ops:
            add(d)
        o = Op(eng, fn, list(deps.values()), lane=lane)
        o.idx = self.n
        self.n += 1
        self.ops[eng].append(o)
        if lane is not None:
            self.lanes.setdefault(lane, []).append(o)
        for d in o.deps:
            d.needed = True
        for k in reads:
            self.readers.setdefault(k, {})[self._sig(o)] = o
        for k in writes:
            self.last_writer[k] = o
            self.readers[k] = {}
        return o

    def barrier(self):
        b = []
        for e in self.ENGS:
            comp = [o for o in self.ops[e] if not o.is_dma]
            if comp:
                b.append(comp[-1])
        for lst in self.lanes.values():
            b.append(lst[-1])
        for o in b:
            o.needed = True
        self.barrier_ops = b

    def emit(self, nc, final_waits=()):
        for o in final_waits:
            o.needed = True
        for e in self.ENGS:
            c = 0
            for o in self.ops[e]:
                if not o.is_dma and o.needed:
                    c += 1
                    o.token = c
        for lst in self.lanes.values():
            c = 0
            for o in lst:
                c += 16
                o.token = c
        sem_names = list(self.ENGS[:4]) + sorted(self.lanes.keys())
        with contextlib.ExitStack() as st:
            sems = {n: st.enter_context(nc.semaphore("s_" + n)) for n in sem_names}
            block = st.enter_context(nc.Block())
            hw = {"pe": block.tensor, "act": block.scalar, "dve": block.vector,
                  "pool": block.gpsimd, "sp": block.sync}

            def make(ekey):
                def body(eng):
                    waited = {}

                    def wait(d):
                        sk = self._sig(d)
                        if waited.get(sk, 0) < d.token:
                            eng.wait_ge(sems[sk], d.token)
                            waited[sk] = d.token

                    for o in self.ops[ekey]:
                        for d in o.deps:
                            wait(d)
                        ins = o.fn(eng)
                        if o.is_dma:
                            ins.then_inc(sems[o.lane], 16)
                        elif o.needed:
                            ins.then_inc(sems[ekey], 1)
                    if ekey == "sp":
                        for d in final_waits:
                            wait(d)
                return body

            for ekey in self.ENGS:
                if self.ops[ekey] or ekey == "sp":
                    hw[ekey](make(ekey))


class Alloc:
    def __init__(self, nc, base, top, tag):
        self.nc, self.p, self.top, self.tag = nc, base, top, tag
        self.i = 0

    def a(self, name, shape, dt):
        esz = 2 if dt == BF16 else 4
        n = esz
        for s in shape[1:]:
            n *= s
        off = (self.p + 63) // 64 * 64
        assert off + n <= self.top, f"SBUF overflow in {self.tag}: {name} needs {off + n} > {self.top}"
        self.p = off + n
        self.i += 1
        return self.nc.alloc_sbuf_tensor_at(f"{self.tag}_{name}", list(shape), dt, offset=off)


def build_program(stop=None, dumps=()):
    nc = bass.Bass("TRN2", target_bir_lowering=False)
    DBG = {}

    def finish(loc):
        fin = []
        for i, name in enumerate(dumps):
            t = loc[name]
            shp = list(t.shape)
            flat = 1
            for v in shp[1:]:
                flat *= v
            dd = nc.dram_tensor("dbg_" + name, [shp[0], flat], t.dtype, kind="ExternalOutput").ap()
            src = t[:]
            if len(shp) == 3:
                src = src.rearrange("p a b -> p (a b)")
            elif len(shp) == 4:
                src = src.rearrange("p a b c -> p (a b c)")
            fin.append(p.op("sp", dma(dd, src), lane=f"dbg{i}", force=True))
        p.emit(nc, final_waits=fin)
        return nc


    def din(name, shape):
        return nc.dram_tensor(name, list(shape), F32, kind="ExternalInput").ap()

    xr = din("xr", [4096, D])
    metap = din("metap", [128, D])
    w_in = din("w_in", [D, 2312])
    w_out = din("w_out", [D, D])
    w_gu = din("w_gu", [NF * 128, 2048])
    w_dn = din("w_dn", [DFF, D])
    g_pre_mix = din("g_pre_mix", [128, D])
    g_post_mix = din("g_post_mix", [128, D])
    g_pre_ffn = din("g_pre_ffn", [128, D])
    g_post_ffn = din("g_post_ffn", [128, D])
    bfb_d = din("bfb", [128, 32])
    sinkb_d = din("sinkb", [128, 8])
    ebw_d = din("ebw", [128, 2048])
    ebm0_d = din("ebm0", [128, 1024])
    ebmc_d = din("ebmc", [128, 8])
    mask_d = din("maskdiag", [128, 128])
    ident_d = din("ident", [128, 128])
    tri_d = din("tri", [128, 128])
    ones_d = din("ones", [128, 128])
    sel_d = din("sel63", [128, 128])
    cst_d = din("cst", [128, 8])
    out = nc.dram_tensor("out", [NS * 128, D], F32, kind="ExternalOutput").ap()
    wgu_bf = nc.dram_tensor("wgu_bf", [NF * 128, 2048], BF16, kind="Internal").ap()

    p = Prog()
    import os
    if os.environ.get("KLIMIT"):
        p.limit = int(os.environ["KLIMIT"])
    BASE, TOP = 16512, 229344

    lo = Alloc(nc, BASE, TOP, "lo")
    identb = lo.a("identb", [128, 128], BF16)
    cst = lo.a("cst", [128, 8], F32)
    ss = lo.a("ss", [128, 64], F32)
    ms = lo.a("ms", [128, 64], F32)
    rstd = lo.a("rstd", [128, 64], F32)
    junk = lo.a("junk", [128, D], BF16)
    LO_END = lo.p

    MIXT_OFF = (TOP - 8 * 2048 * 2) // 64 * 64
    mixT = nc.alloc_sbuf_tensor_at("mixT", [128, 8, 2048], BF16, offset=MIXT_OFF)

    st = Alloc(nc, LO_END, MIXT_OFF, "st")
    KTf = st.a("KTf", [128, 4, NT * 128], BF16)
    QTf = st.a("QTf", [128, 4, NS * 128], BF16)
    Vf = st.a("Vf", [128, NT, 8, 65], BF16)
    KTs = st.a("KTs", [128, NT * 128], BF16)
    QTs = st.a("QTs", [128, 4, NS * 128], BF16)
    Vs = st.a("Vs", [128, NT, 2, 65], BF16)
    maskb = st.a("maskb", [128, 128], BF16)
    tri = st.a("tri", [128, 128], F32)
    ones = st.a("ones", [128, 128], F32)
    sel63 = st.a("sel63", [128, 128], F32)
    bfb = st.a("bfb", [128, 32], F32)
    esink = st.a("esink", [128, 8], F32)
    ebmc = st.a("ebmc", [128, 8], F32)
    fb_all = st.a("fb_all", [128, NT, 8], F32)
    e_all = st.a("e_all", [128, NT, 8], F32)
    L_all = st.a("L_all", [128, NT, 8], F32)
    tot_sb = st.a("tot_sb", [128, NT, 8], F32)
    incl = st.a("incl", [128, NT, 8], F32)
    cumL = st.a("cumL", [128, NT, 8], F32)
    cref = st.a("cref", [128, NS, 8], F32)
    ST_END = st.p

    r1 = Alloc(nc, ST_END, TOP, "r1")
    winb = r1.a("winb", [128, 8, 2312], BF16)
    xs = [r1.a(f"xs{i}", [128, D], F32) for i in range(3)]
    hn_tok = [r1.a(f"hntok{i}", [128, D], BF16) for i in range(2)]
    hnT = [r1.a(f"hnT{i}", [128, 8, 512], BF16) for i in range(2)]
    gpre = r1.a("gpre", [128, D], F32)

    r2 = Alloc(nc, ST_END, MIXT_OFF, "r2")
    PT = [r2.a(f"PT{i}", [128, 1024], BF16) for i in range(2)]
    Vp = [r2.a(f"Vp{i}", [128, NT, 65], BF16) for i in range(4)]
    aj = [r2.a(f"aj{i}", [128, NT, 8], F32) for i in range(2)]
    ebw = r2.a("ebw", [128, 2, 8, 128], F32)
    ebm0 = r2.a("ebm0", [128, 8, 128], F32)
    maskn2 = nc.alloc_sbuf_tensor_at("maskn2", [128, 256], BF16, offset=tri.manual_sbuf_range[0])
    PT.append(nc.alloc_sbuf_tensor_at("PT2", [128, 1024], BF16, offset=fb_all.manual_sbuf_range[0]))
    assert e_all.manual_sbuf_range[1] - fb_all.manual_sbuf_range[0] >= 2048
    _eo = ebm0.manual_sbuf_range[0]
    qpad = [nc.alloc_sbuf_tensor_at(f"qpad{i}", [128, 8, 128], BF16, offset=_eo + 2048 * i) for i in range(2)]
    Ef = r2.a("Ef", [128, 1024], F32)
    Efm = r2.a("Efm", [128, 512], F32)
    PTs = r2.a("PTs", [128, 1024], BF16)
    PTm = r2.a("PTm", [128, 512], BF16)
    mix_tok = [r2.a(f"mixtok{i}", [128, D], BF16) for i in range(2)]
    mix_tok.append(nc.alloc_sbuf_tensor_at("mixtok2", [128, D], BF16, offset=tot_sb.manual_sbuf_range[0]))
    assert incl.manual_sbuf_range[1] - tot_sb.manual_sbuf_range[0] >= 2048
    den = r2.a("den", [128, 8], F32)
    rec = r2.a("rec", [128, 8], F32)

    SWA_LO, SWA_HI = KTs.manual_sbuf_range[0], Vs.manual_sbuf_range[1]
    r3a = Alloc(nc, SWA_LO, SWA_HI, "r3a")
    woutb = r3a.a("woutb", [128, 8, D], BF16)
    gpm = r3a.a("gpm", [128, D], F32)
    gpf = r3a.a("gpf", [128, D], F32)
    gqf = r3a.a("gqf", [128, D], F32)
    xres = r3a.a("xres", [128, D], F32)
    rA = Alloc(nc, LO_END, SWA_LO, "rA")
    wdnb = rA.a("wdnb", [128, NF, D], BF16)
    h1 = [rA.a(f"h1_{i}", [128, 4, D], F32) for i in range(2)]
    asb = rA.a("asb", [128, D], F32)
    rB = Alloc(nc, (SWA_HI + 63) // 64 * 64, MIXT_OFF, "rB")
    actT = rB.a("actT", [128, NF, 512], BF16)
    hn2T = [rB.a(f"hn2T{i}", [128, 8, 512], BF16) for i in range(2)]
    wgub = [rB.a(f"wgub{i}", [128, 8, 256], BF16) for i in range(3)]
    esb = [rB.a(f"esb{i}", [128, 512], F32) for i in range(2)]
    hn2_tok = [rA.a("hn2tok0", [128, D], BF16), rB.a("hn2tok1", [128, D], BF16)]
    fsb = asb

    SA = nc.alloc_psum_tensor("SA", [128, 1024], F32)
    SB = nc.alloc_psum_tensor("SB", [128, 1024], F32)
    OA0 = nc.alloc_psum_tensor("OA0", [128, 512], F32)
    OA1 = nc.alloc_psum_tensor("OA1", [128, 512], F32)
    OB0 = nc.alloc_psum_tensor("OB0", [128, 512], F32)
    OB1 = nc.alloc_psum_tensor("OB1", [128, 512], F32)
    O0 = OA0
    T = OB1[:].bitcast(BF16)
    banks = [SA[:, 0:512], SA[:, 512:1024], SB[:, 0:512], SB[:, 512:1024], OA0[:], OA1[:], OB0[:]]
    OPAIR = [(OA0, OA1), (OB0, OB1)]
    PK = [("PA0", "PA1"), ("PB0", "PB1")]
    rot = [0]

    def next_bank():
        i = rot[0] % 7
        rot[0] += 1
        return banks[i], f"b{i}"

    evc = [0]

    def evac_eng():
        evc[0] += 1
        return "act" if evc[0] % 2 else "dve"

    def copy_fn(eng, dst, src):
        if eng == "act":
            return lambda e: e.activation(out=dst, in_=src, func=AF.Copy)
        return lambda e: e.tensor_copy(out=dst, in_=src)

    def dma(dst, src):
        return lambda e: e.dma_start(out=dst, in_=src)

    def mm_group(outap, pairs):
        def f(e):
            ins = None
            n = len(pairs)
            for i, (l, r) in enumerate(pairs):
                ins = e.matmul(outap, lhsT=l, rhs=r, start=(i == 0), stop=(i == n - 1))
            return ins
        return f

    def transposes(src_tok, keyreads, Tap=None, tkey="T"):
        Tap = T if Tap is None else Tap

        def f(e):
            ins = None
            for c in range(8):
                ins = e.transpose(Tap[:, c * 128:(c + 1) * 128], src_tok[:, c * 128:(c + 1) * 128], identb[:])
            return ins
        return p.op("pe", f, reads=list(keyreads) + ["identb"], writes=[tkey])

    Tv = T[:].rearrange("p (c r) -> p c r", r=128)

    def rstd_ops(col, key):
        p.op("act", lambda e: e.activation(out=ms[:, col:col + 1], in_=ss[:, col:col + 1], func=AF.Ln,
                                           scale=1.0 / D, bias=EPS),
             reads=[f"ss{key}"], writes=[f"ms{key}"])
        p.op("act", lambda e: e.activation(out=rstd[:, col:col + 1], in_=ms[:, col:col + 1], func=AF.Exp, scale=-0.5),
             reads=[f"ms{key}"], writes=[f"rstd{key}"])

    p.op("sp", dma(cst[:], cst_d), writes=["cst"], lane="c0")
    p.op("sp", dma(gpre[:], g_pre_mix), writes=["gpre"], lane="c1")
    p.op("sp", dma(bfb[:], bfb_d), writes=["bfb"], lane="c2")
    p.op("sp", dma(tri[:], tri_d), writes=["tri"], lane="c3")
    p.op("sp", dma(ones[:], ones_d), writes=["ones"], lane="c3")
    p.op("sp", dma(sel63[:], sel_d), writes=["sel63"], lane="c3")
    p.op("sp", dma(esink[:], sinkb_d), writes=["esink"], lane="c3")
    lastc3 = p.op("sp", dma(ebmc[:], ebmc_d), writes=["ebmc"], lane="c3")
    for k in ("tri", "ones", "sel63", "esink", "ebmc"):
        p.last_writer[k] = lastc3
    p.op("pool", dma(identb[:], ident_d), writes=["identb"], lane="c4")
    p.op("pool", dma(maskb[:], mask_d), writes=["maskb"], lane="c5")
    w_in_v = w_in.rearrange("(c p) n -> p c n", p=128)
    WGRP = {"kb": (C_KB, C_KB + 512), "kavaf": (C_KA, C_QB), "vb": (C_VB, C_VB + 512),
            "qb": (C_QB, C_QB + 512), "qa": (C_QA, C_QA + 512)}
    for gi_, (gname, (c0, c1)) in enumerate(WGRP.items()):
        p.op("pool", dma(winb[:, :, c0:c1], w_in_v[:, :, c0:c1]), writes=[f"win_{gname}"], lane=f"w{3 + gi_}")
    WIN = []
    conv_gate = []

    def emit_wgu_conversion():
        for f in range(NF):
            p.op("pool", dma(wgu_bf[f * 128:(f + 1) * 128, :], w_gu[f * 128:(f + 1) * 128, :]), writes=["wgu_bf"],
                 lane="w2", extra=conv_gate)
        p.last_writer["wgu_bf"] = p.lanes["w2"][-1]
    p.op("dve", lambda e: e.memset(Vf[:].rearrange("p t h d -> p (t h) d")[:, :, 64:65], 1.0), writes=["Vf1"])
    p.op("dve", lambda e: e.memset(Vs[:].rearrange("p t h d -> p (t h) d")[:, :, 64:65], 1.0), writes=["Vs1"])
    p.op("dve", lambda e: e.tensor_scalar(out=Vf[:, 0, :, 64:65], in0=Vf[:, 0, :, 64:65], scalar1=cst[:, 2:3],
                                          scalar2=None, op0=ALU.mult), reads=["Vf1", "cst"], writes=["Vf1"])
    p.op("dve", lambda e: e.tensor_scalar(out=Vs[:, 0, :, 64:65], in0=Vs[:, 0, :, 64:65], scalar1=cst[:, 2:3],
                                          scalar2=None, op0=ALU.mult), reads=["Vs1", "cst"], writes=["Vs1"])


    if stop == 'p0':
        p.barrier()
        return finish(locals())
    cnt = {"xs": 0, "hb": 0}

    def group_blocks(gi):
        if gi == 0:
            return [(0, None)]
        return [(4 * (gi - 1) + bi + 1, 4 * (gi - 1) + bi) for bi in range(4)]

    def A_steps(gi):
        blocks = group_blocks(gi)
        gb = gi % 2
        st8 = {}

        def front(bi):
            def f():
                t, xb = blocks[bi]
                s_ = cnt["xs"] % 3
                cnt["xs"] += 1
                hb = cnt["hb"] % 2
                cnt["hb"] += 1
                st8[bi] = hb
                src = metap if xb is None else xr[xb * 128:(xb + 1) * 128, :]
                p.op("sp", dma(xs[s_][:], src), writes=[f"xs{s_}"], lane=f"xs{s_}")
                p.op("act", lambda e: e.activation(out=junk[:], in_=xs[s_][:], func=AF.Square,
                                                   accum_out=ss[:, t:t + 1]),
                     reads=[f"xs{s_}"], writes=[f"ss{t}"])
                rstd_ops(t, t)
                p.op("dve", lambda e: e.scalar_tensor_tensor(
                    out=hn_tok[hb][:], in0=xs[s_][:], scalar=rstd[:, t:t + 1], in1=gpre[:],
                    op0=ALU.mult, op1=ALU.mult),
                     reads=[f"xs{s_}", f"rstd{t}", "gpre"], writes=[f"hntok{hb}"])
            return f

        def tr(bi):
            def f():
                hb = st8[bi]
                transposes(hn_tok[hb], [f"hntok{hb}"])
                p.op("act", copy_fn("act", hnT[gb][:, :, bi * 128:(bi + 1) * 128], Tv), reads=["T"],
                     writes=[f"hnT{gb}:{bi}"])
            return f
        n = len(blocks)
        if n == 1:
            return [front(0), tr(0)]
        return [front(0), front(1), tr(0), front(2), tr(1), front(3), tr(2), tr(3)]

    def B_steps(gi):
        blocks = group_blocks(gi)
        nb = len(blocks)
        N = nb * 128
        gb = gi % 2
        t0 = blocks[0][0]
        hk = [f"hnT{gb}:{bi}" for bi in range(nb)]
        steps = []

        def proj_T(col0, rhs_fn, ncols, dst, three=False, wk="kb"):
            def f():
                bank, bk = next_bank()
                pairs = [(winb[:, c, col0:col0 + 128], rhs_fn(c)) for c in range(8)]
                oap = bank[:, 0:ncols]
                if three:
                    oap = oap.rearrange("p (b r) -> p b r", r=128)
                o_ = p.op("pe", mm_group(oap, pairs), reads=hk + [f"win_{wk}"], writes=[bk])
                if gi == 2 and not conv_gate:
                    conv_gate.append(o_)
                    emit_wgu_conversion()
                eng = evac_eng()
                p.op(eng, copy_fn(eng, dst, bank[:, 0:ncols]), reads=[bk])
            return f

        full = lambda c: hnT[gb][:, c, 0:N]
        for i in range(4):
            steps.append(proj_T(C_KB + i * 128, full, N, KTf[:, i, t0 * 128:t0 * 128 + N]))
        steps.append(proj_T(C_KA, full, N, KTs[:, t0 * 128:t0 * 128 + N], wk="kavaf"))
        if gi >= 1:
            own = lambda c: hnT[gb][:, c, :].rearrange("p (b two r) -> p b two r", two=2, r=128)[:, :, 1, :]
            j0 = 2 * (gi - 1)
            for i in range(4):
                steps.append(proj_T(C_QB + i * 128, own, 256, QTf[:, i, j0 * 128:(j0 + 2) * 128], three=True, wk="qb"))
            for i in range(4):
                steps.append(proj_T(C_QA + i * 128, own, 256, QTs[:, i, j0 * 128:(j0 + 2) * 128], three=True, wk="qa"))

        def vproj(bi, t):
            lt = lambda c: hnT[gb][:, c, bi * 128:(bi + 1) * 128]

            def f1():
                bank, bk = next_bank()
                p.op("pe", mm_group(bank[:, 0:512], [(lt(c), winb[:, c, C_VB:C_VB + 512]) for c in range(8)]),
                     reads=hk + ["win_vb"], writes=[bk])
                p.op("act", copy_fn("act", Vf[:, t, :, 0:64], bank[:, 0:512].rearrange("p (h d) -> p h d", d=64)),
                     reads=[bk, "Vf1"])

            def f2():
                bank, bk = next_bank()
                p.op("pe", mm_group(bank[:, 0:136], [(lt(c), winb[:, c, C_VA:C_VA + 136]) for c in range(8)]),
                     reads=hk + ["win_kavaf"], writes=[bk])
                p.op("act", copy_fn("act", Vs[:, t, :, 0:64], bank[:, 0:128].rearrange("p (h d) -> p h d", d=64)),
                     reads=[bk, "Vs1"])
                p.op("dve", lambda e: e.tensor_tensor(out=fb_all[:, t, :], in0=bank[:, 128:136],
                                                      in1=bfb[:, 0:8], op=ALU.add),
                     reads=[bk, "bfb"], writes=[f"fb{t}"])
            return [f1, f2]
        for bi, (t, xb) in enumerate(blocks):
            steps.extend(vproj(bi, t))

        def lops():
            fk = [f"fb{t}" for t, _ in blocks]
            p.op("act", lambda e: e.activation(out=e_all[:, t0:t0 + nb, :], in_=fb_all[:, t0:t0 + nb, :],
                                               func=AF.Exp, scale=-1.0),
                 reads=fk, writes=[f"eL{gi}"])
            p.op("act", lambda e: e.activation(out=L_all[:, t0:t0 + nb, :], in_=e_all[:, t0:t0 + nb, :],
                                               func=AF.Ln, bias=1.0),
                 reads=[f"eL{gi}"], writes=[f"L{gi}"])
        steps.append(lops)
        return steps

    for stp in A_steps(0):
        stp()
    for gi in range(9):
        Bs = B_steps(gi)
        As = A_steps(gi + 1) if gi + 1 <= 8 else []
        nB, nA = len(Bs), len(As)
        ai = 0
        for bi_, b in enumerate(Bs):
            b()
            while ai < nA and (ai + 1) * nB <= (bi_ + 1) * nA:
                As[ai]()
                ai += 1
        while ai < nA:
            As[ai]()
            ai += 1

    if stop == 'p1':
        p.barrier()
        return finish(locals())
    p.barrier()
    p.op("sp", dma(ebw[:].rearrange("p a h q -> p (a h q)"), ebw_d), writes=["ebw"], lane="c1")
    p.op("sp", dma(ebm0[:].rearrange("p h q -> p (h q)"), ebm0_d), writes=["ebm0a", "ebm0b"], lane="c2")
    p.op("act", lambda e: e.activation(out=ebw[:].rearrange("p a h q -> p (a h q)"),
                                       in_=ebw[:].rearrange("p a h q -> p (a h q)"), func=AF.Exp),
         reads=["ebw"], writes=["ebw"])
    p.op("act", lambda e: e.activation(out=ebm0[:].rearrange("p h q -> p (h q)"),
                                       in_=ebm0[:].rearrange("p h q -> p (h q)"), func=AF.Exp),
         reads=["ebm0a", "ebm0b"], writes=["ebm0a", "ebm0b"])
    p.op("act", lambda e: e.activation(out=esink[:], in_=esink[:], func=AF.Exp), reads=["esink"], writes=["esink"])
    p.op("act", lambda e: e.activation(out=ebmc[:], in_=ebmc[:], func=AF.Exp), reads=["ebmc"], writes=["ebmc"])

    p.op("dve", lambda e: e.tensor_scalar(out=L_all[:, 0, :], in0=L_all[:, 0, :], scalar1=cst[:, 2:3], scalar2=None,
                                          op0=ALU.mult), reads=["cst"], writes=["L"])
    p.op("dve", lambda e: e.tensor_scalar(out=L_all[:, 1, :], in0=L_all[:, 1, :], scalar1=cst[:, 0:1], scalar2=None,
                                          op0=ALU.mult), reads=["cst", "L"], writes=["L"])
    Lflat = L_all[:].rearrange("p t h -> p (t h)")
    p.op("pe", lambda e: e.matmul(SA[:, 0:NT * 8], lhsT=tri[:], rhs=Lflat, start=True, stop=True),
         reads=["L", "tri"], writes=["SA"])
    p.op("pe", lambda e: e.matmul(SB[:, 0:NT * 8], lhsT=ones[:], rhs=Lflat, start=True, stop=True),
         reads=["L", "ones"], writes=["SB"])
    p.op("dve", lambda e: e.tensor_copy(out=tot_sb[:].rearrange("p t h -> p (t h)"), in_=SB[:, 0:NT * 8]),
         reads=["SB"], writes=["tot"])
    p.op("dve", lambda e: e.memset(aj[0][:], 1.0), writes=["aj0"])
    for h in range(8):
        p.op("dve", lambda e, h=h: e.tensor_tensor_scan(out=incl[:, :, h], data0=aj[0][:, :, h], data1=tot_sb[:, :, h],
                                                        initial=0.0, op0=ALU.mult, op1=ALU.add),
             reads=["tot", "aj0"], writes=[f"incl{h}"])
    p.op("dve", lambda e: e.tensor_tensor(out=cumL[:].rearrange("p t h -> p (t h)"), in0=SA[:, 0:NT * 8],
                                          in1=incl[:].rearrange("p t h -> p (t h)"), op=ALU.add),
         reads=["SA"] + [f"incl{h}" for h in range(8)], writes=["cumL"])
    p.op("dve", lambda e: e.tensor_tensor(out=cumL[:], in0=cumL[:], in1=tot_sb[:], op=ALU.subtract),
         reads=["cumL", "tot"], writes=["cumL"])
    p.op("dve", lambda e: e.tensor_scalar(out=cumL[:, 1, :], in0=cumL[:, 1, :], scalar1=cst[:, 1:2], scalar2=None,
                                          op0=ALU.add), reads=["cumL", "cst"], writes=["cumL"])
    cum_own = cumL[:, 1:NT, :].rearrange("p (j two) h -> p j two h", two=2)[:, :, 1, :]
    p.op("pe", lambda e: e.matmul(O0[:, 0:128].rearrange("p (j h) -> p j h", h=8), lhsT=sel63[:], rhs=cum_own,
                                  start=True, stop=True),
         reads=["cumL", "sel63"], writes=["PA0"])
    p.op("dve", lambda e: e.tensor_copy(out=cref[:].rearrange("p j h -> p (j h)"), in_=O0[:, 0:128]),
         reads=["PA0"], writes=["cref"])

    if stop == 'p15':
        p.barrier()
        return finish(locals())
    def Oreg_f(j, h):
        return OPAIR[j % 2][h % 2][:, (h // 2) * 65:(h // 2 + 1) * 65]

    pending_tr = [None]
    sbuf_i = [0]
    pt_i = [0]
    vp_i = [0]
    vp_of = {}
    swa_last_pe = {}
    WOUT = [f"wout{c}" for c in range(8)]

    def prefetch_phase3():
        dep = [swa_last_pe[NS - 1]]
        stg = [gpm, gpf, gqf]
        for c in range(8):
            sb_ = stg[c % 3]
            p.op("sp", dma(sb_[:], w_out[c * 128:(c + 1) * 128, :]), writes=[f"stg{c % 3}"], lane=f"c{c % 3}",
                 extra=dep)
            p.op("dve", lambda e, sb_=sb_, c=c: e.tensor_copy(out=woutb[:, c, :], in_=sb_[:]),
                 reads=[f"stg{c % 3}"], writes=[f"wout{c}"])
        p.op("sp", dma(gpm[:], g_post_mix), writes=["stg0", "gpm"], lane="c0")
        p.op("sp", dma(gpf[:], g_pre_ffn), writes=["stg1", "gpf"], lane="c1")
        p.op("sp", dma(gqf[:], g_post_ffn), writes=["stg2", "gqf"], lane="c2")

    def next_S():
        sb = sbuf_i[0] % 2
        sbuf_i[0] += 1
        return (SA, "SA") if sb == 0 else (SB, "SB")

    def a_step_dve(j):
        nt = 2 * j + 3
        ab = j % 2
        p.op("dve", lambda e: e.tensor_tensor(
            out=aj[ab][:, 0:nt, :], in0=cumL[:, 0:nt, :],
            in1=cref[:, j, :].unsqueeze(1).broadcast_to([128, nt, 8]), op=ALU.subtract),
             reads=["cumL", "cref"], writes=[f"aj{ab}"])

    def a_step_act(j):
        nt = 2 * j + 3
        ab = j % 2
        p.op("act", lambda e: e.activation(out=aj[ab][:, 0:nt, :], in_=aj[ab][:, 0:nt, :], func=AF.Exp),
             reads=[f"aj{ab}"], writes=[f"aj{ab}"])

    def a_step(j):
        a_step_dve(j)
        a_step_act(j)

    VCH = 8

    def ensure_vp(j, h):
        if (j, h) in vp_of or j >= NS:
            return
        nt = 2 * j + 3
        ab = j % 2
        vb = 2 * ((4 * j + h // 2) % 2) + (h % 2)
        vp_of[(j, h)] = vb
        eng = "dve" if h % 2 == 0 else "pool"
        for ck, (ta, tb_) in (("a", (0, min(VCH, nt))), ("b", (VCH, nt))):
            if tb_ <= ta:
                continue
            n_ = tb_ - ta
            p.op(eng, lambda e, ta=ta, tb_=tb_, n_=n_: e.tensor_tensor(
                out=Vp[vb][:, ta:tb_, :], in0=Vf[:, ta:tb_, h, :],
                in1=aj[ab][:, ta:tb_, h].unsqueeze(2).broadcast_to([128, n_, 65]), op=ALU.mult),
                 reads=[f"aj{ab}", "Vf1"], writes=[f"Vp{vb}:{ck}"])

    def qpad_step(j):
        qb = j % 2
        qv = qpad[qb][:].rearrange("p (i two) q -> p i two q", two=2)
        p.op("dve", lambda e: e.tensor_copy(out=qv[0:64, :, 0, :], in_=QTf[0:64, :, j * 128:(j + 1) * 128]),
             writes=[f"qpad{qb}"])
        p.op("dve", lambda e: e.tensor_copy(out=qv[64:128, :, 1, :], in_=QTf[64:128, :, j * 128:(j + 1) * 128]),
             writes=[f"qpad{qb}"])

    def swa_steps(j):
        mb = j % 3
        pidx = 1 if j == 0 else j % 2
        tq = 2 * j + 2
        tp = 2 * j + 1
        steps = []
        v3 = lambda a: a.rearrange("p (i q) -> p i q", q=128)
        for g in range(2):
            gs = slice(g * 64, (g + 1) * 64)
            qrhs = QTs[gs, :, j * 128:(j + 1) * 128]

            def sA(g=g, gs=gs, qrhs=qrhs):
                Sb, sk = next_S()

                def f(e):
                    e.matmul(v3(Sb[:, 0:512]), lhsT=KTs[gs, tp * 128:(tp + 1) * 128], rhs=qrhs, start=True, stop=True)
                    return e.matmul(v3(Sb[:, 512:1024]), lhsT=KTs[gs, tq * 128:(tq + 1) * 128], rhs=qrhs,
                                    start=True, stop=True)
                p.op("pe", f, writes=[sk])
                p.op("act", lambda e: e.activation(out=Ef[:], in_=Sb[:], func=AF.Exp, scale=0.125),
                     reads=[sk], writes=["Ef"])
                p.op("dve", lambda e: e.tensor_tensor(
                    out=PTs[:].rearrange("p (a i q) -> p a i q", a=2, i=4),
                    in0=Ef[:].rearrange("p (a i q) -> p a i q", a=2, i=4),
                    in1=ebw[:, :, g * 4:(g + 1) * 4, :], op=ALU.mult),
                     reads=["Ef", "ebw"], writes=["PTs"])
                if j == 0:
                    p.op("dve", lambda e: e.tensor_scalar(out=PTs[:, 0:512], in0=PTs[:, 0:512], scalar1=cst[:, 0:1],
                                                          scalar2=None, op0=ALU.mult),
                         reads=["PTs", "cst"], writes=["PTs"])

            def sB(g=g, gs=gs, qrhs=qrhs):
                Sb, sk = next_S()
                p.op("pe", lambda e: e.matmul(v3(Sb[:, 0:512]), lhsT=KTs[gs, 0:128], rhs=qrhs, start=True, stop=True),
                     writes=[sk])
                p.op("act", lambda e: e.activation(out=Efm[:], in_=Sb[:, 0:512], func=AF.Exp, scale=0.125),
                     reads=[sk], writes=["Efm"])
                if j == 0:
                    in1 = ebm0[:, g * 4:(g + 1) * 4, :]
                    rk = "ebm0a" if g == 0 else "ebm0b"
                else:
                    in1 = ebmc[:, g * 4:(g + 1) * 4].unsqueeze(2).broadcast_to([128, 4, 128])
                    rk = "ebmc"
                p.op("dve", lambda e: e.tensor_tensor(
                    out=PTm[:].rearrange("p (i q) -> p i q", i=4),
                    in0=Efm[:].rearrange("p (i q) -> p i q", i=4), in1=in1, op=ALU.mult),
                     reads=["Efm", rk], writes=["PTm"])

            def sPV(g=g):
                def f(e):
                    ins = None
                    for i in range(4):
                        o = OPAIR[pidx][0][:, i * 65:(i + 1) * 65]
                        e.matmul(o, lhsT=PTs[:, i * 128:(i + 1) * 128], rhs=Vs[:, tp, g, :], start=True, stop=False)
                        e.matmul(o, lhsT=PTs[:, 512 + i * 128:512 + (i + 1) * 128], rhs=Vs[:, tq, g, :],
                                 start=False, stop=False)
                        ins = e.matmul(o, lhsT=PTm[:, i * 128:(i + 1) * 128], rhs=Vs[:, 0, g, :],
                                       start=False, stop=True)
                    return ins
                xk = PK[pidx][0]
                swa_last_pe[j] = p.op("pe", f, reads=["PTs", "PTm", "Vs1"], writes=[xk])
                Ov = OPAIR[pidx][0][:, 0:260].rearrange("p (i d) -> p i d", d=65)
                p.op("dve", lambda e: e.tensor_tensor(out=den[:, 0:4], in0=Ov[:, :, 64],
                                                      in1=esink[:, g * 4:(g + 1) * 4], op=ALU.add),
                     reads=[xk, "esink"], writes=["den"])
                p.op("dve", lambda e: e.reciprocal(out=rec[:, 0:4], in_=den[:, 0:4]), reads=["den"], writes=["rec"])
                p.op("dve", lambda e: e.tensor_tensor(
                    out=mix_tok[mb][:, g * 256:(g + 1) * 256].rearrange("p (i d) -> p i d", d=64),
                    in0=Ov[:, :, 0:64], in1=rec[:, 0:4].unsqueeze(2).broadcast_to([128, 4, 64]), op=ALU.mult),
                     reads=[xk, "rec"], writes=[f"mixtok{mb}:s{g}"])
            steps += [sA, sB, sPV]
        return steps

    SL = []
    for j in range(NS):
        nt = 2 * j + 3
        ptiles = [(i, t) for i in range(4) for t in range(nt)]
        batches = [ptiles[x:x + 4] for x in range(0, len(ptiles), 4)]
        SL.append({"nt": nt, "tq": 2 * j + 2, "batches": batches, "nbt": len(batches), "touched": set(), "ei": 0,
                   "extras": [], "late": None})
    stream = [(j, b) for j in range(NS) for b in range(SL[j]["nbt"])]
    a_done = {0}
    pref_hi = [1]
    pend_vp = []
    tr_due = {}
    pb_of = {}

    def try_prefetch():
        for (jt, it) in list(pend_vp):
            if jt >= NS:
                pend_vp.remove((jt, it))
            elif jt in a_done:
                ensure_vp(jt, 2 * it)
                ensure_vp(jt, 2 * it + 1)
                pend_vp.remove((jt, it))

    def emit_pv(g):
        j, b = stream[g]
        sl = SL[j]
        batch = sl["batches"][b]
        nt = sl["nt"]
        pb = pb_of[g]
        hs = sorted(set(h for i, _ in batch for h in (2 * i, 2 * i + 1)))
        for h in hs:
            ensure_vp(j, h)

        def f(e):
            ins = None
            for u, (i, t) in enumerate(batch):
                for par in range(2):
                    h = 2 * i + par
                    ins = e.matmul(Oreg_f(j, h), lhsT=PT[pb][:, u * 256 + par * 128:u * 256 + (par + 1) * 128],
                                   rhs=Vp[vp_of[(j, h)]][:, t, :], start=(t == 0), stop=(t == nt - 1))
            return ins
        wk = [f"Of{j % 2}:{h}" for h in hs]
        for bk_ in (0, 1):
            if bk_ not in sl["touched"]:
                sl["touched"].add(bk_)
                wk.append(PK[j % 2][bk_])
        vkeys = sorted(set(f"Vp{vp_of[(j, 2 * i + par)]}:{'a' if t < VCH else 'b'}"
                           for i, t in batch for par in range(2)))
        p.op("pe", f, reads=[f"PT{pb}"] + vkeys, writes=wk)
        if b == sl["nbt"] - 1:
            finish_slot(j, g)

    def finish_slot(j, g):
        mb = j % 3
        for par in range(2):
            Ov = OPAIR[j % 2][par][:, 0:260].rearrange("p (i d) -> p i d", d=65)
            okeys = [f"Of{j % 2}:{h}" for h in range(par, 8, 2)]
            p.op("dve", lambda e, Ov=Ov: e.reciprocal(out=rec[:, 4:8], in_=Ov[:, :, 64]),
                 reads=okeys + [PK[j % 2][par]], writes=["rec2"])
            p.op("dve", lambda e, Ov=Ov, par=par: e.tensor_tensor(
                out=mix_tok[mb][:, 512:1024].rearrange("p (i two d) -> p i two d", two=2, d=64)[:, :, par, :],
                in0=Ov[:, :, 0:64], in1=rec[:, 4:8].unsqueeze(2).broadcast_to([128, 4, 64]), op=ALU.mult),
                 reads=okeys + ["rec2", PK[j % 2][par]], writes=[f"mixtok{mb}:f{par}"])

        def do_tr():
            Tj = OPAIR[j % 2][1][:].bitcast(BF16)
            tk = PK[j % 2][1]
            transposes(mix_tok[mb], [f"mixtok{mb}:s0", f"mixtok{mb}:s1", f"mixtok{mb}:f0", f"mixtok{mb}:f1"],
                       Tap=Tj, tkey=tk)

            def ev(e):
                ins = None
                for c in range(8):
                    ins = e.tensor_copy(out=mixT[:, c, j * 128:(j + 1) * 128], in_=Tj[:, c * 128:(c + 1) * 128])
                return ins
            if j < 6:
                p.op("act", copy_fn("act", mixT[:, :, j * 128:(j + 1) * 128],
                                    Tj.rearrange("p (c r) -> p c r", r=128)), reads=[tk])
            else:
                p.op("dve", ev, reads=[tk])
        tr_due[g + 4] = do_tr

    a_step(0)
    ensure_vp(0, 0)
    ensure_vp(0, 1)
    for stp in swa_steps(0):
        stp()
    for a_ in range(2):
        p.op("dve", lambda e, a_=a_: e.tensor_scalar(out=maskn2[:, a_ * 128:(a_ + 1) * 128], in0=maskb[:], scalar1=-1.0,
                                                     scalar2=240000.0, op0=ALU.add, op1=ALU.mult),
             reads=["maskb"], writes=["tri", "maskn2"])
    for qb_ in range(2):
        p.op("pool", lambda e, qb_=qb_: e.memset(qpad[qb_][:], 0.0),
             writes=["ebm0a" if qb_ == 0 else "ebm0b", f"qpad{qb_}"])
    qpad_step(0)
    for j in range(NS):
        if j + 1 < NS:
            def first_extra(j=j):
                a_step_dve(j + 1)
                qpad_step(j + 1)

            def second_extra(j=j):
                a_step_act(j + 1)
                a_done.add(j + 1)
            sw_ = swa_steps(j + 1)
            SL[j]["extras"] = [first_extra, sw_[0], second_extra] + sw_[1:]
    SL[NS - 1]["late"] = prefetch_phase3
    ensure_vp(0, 0)
    ensure_vp(0, 1)
    pend_vp.append((0, 1))

    G = len(stream)
    for g in range(G):
        j, b = stream[g]
        sl = SL[j]
        batch = sl["batches"][b]
        tq = sl["tq"]
        Sbuf, skey = next_S()

        def fs(e, batch=batch, Sbuf=Sbuf, j=j, tq=tq):
            ins = None
            for u, (i, t) in enumerate(batch):
                diag = (t == tq)
                ins = e.matmul(Sbuf[:, u * 256:(u + 1) * 256].rearrange("p (a q) -> p a q", a=2),
                               lhsT=KTf[:, i, t * 128:(t + 1) * 128],
                               rhs=qpad[j % 2][:, 2 * i:2 * i + 2, :], start=True, stop=not diag)
                if diag:
                    ins = e.matmul(Sbuf[:, u * 256:(u + 1) * 256], lhsT=identb[:], rhs=maskn2[:],
                                   start=False, stop=True)
            return ins
        p.op("pe", fs, reads=[f"qpad{j % 2}", "maskn2", "identb"], writes=[skey])
        if g in tr_due:
            tr_due.pop(g)()
        if g >= 2:
            emit_pv(g - 2)
        pb = g % 3
        pb_of[g] = pb
        n4 = len(batch)
        p.op("act", lambda e, Sbuf=Sbuf, pb=pb, n4=n4: e.activation(
            out=PT[pb][:, 0:n4 * 256], in_=Sbuf[:, 0:n4 * 256], func=AF.Exp, scale=0.125),
             reads=[skey], writes=[f"PT{pb}"])
        if g >= 1:
            j1, b1 = stream[g - 1]
            Pm = 4 * j1 + min(i for i, _ in SL[j1]["batches"][b1])
            while pref_hi[0] < Pm + 1:
                pref_hi[0] += 1
                pend_vp.append(divmod(pref_hi[0], 4))
        try_prefetch()
        if sl["late"] is not None and min(i for i, _ in batch) == 3:
            sl["late"]()
            sl["late"] = None
        ex = sl["extras"]
        while sl["ei"] < len(ex) and (sl["ei"] + 1) * sl["nbt"] <= (b + 1) * (len(ex) + 1):
            ex[sl["ei"]]()
            sl["ei"] += 1
        if b == sl["nbt"] - 1:
            while sl["ei"] < len(ex):
                ex[sl["ei"]]()
                sl["ei"] += 1
            try_prefetch()
    emit_pv(G - 2)
    emit_pv(G - 1)
    for g_ in sorted(tr_due):
        tr_due[g_]()
    tr_due.clear()

    p.barrier()
    if stop == 'p2':
        return finish(locals())
    for f in range(NF):
        p.op("pool", dma(wdnb[:, f, :], w_dn[f * 128:(f + 1) * 128, :]), writes=[f"wdn{f}"], lane="w1")
    lastw = p.lanes["w1"][-1]
    WDN = [f"wdn{f}" for f in range(NF)]
    for k in WDN:
        p.last_writer[k] = lastw


    cnt3 = {"xr": 0, "h2": 0, "wg": 0, "es": 0}

    def col3(j, k):
        return 33 + (3 * j) % 30 + k

    def pre_a(G, tb):
        def f():
            j = 4 * G + tb
            hsel = G % 2
            xb = 2 * j + 1
            p.op("sp", dma(xres[:], xr[xb * 128:(xb + 1) * 128, :]), writes=["xres"], lane="xs0")
            for half in range(2):
                bank, bk = next_bank()
                p.op("pe", mm_group(bank[:, 0:512], [(mixT[:, c, j * 128:(j + 1) * 128],
                                                      woutb[:, c, half * 512:(half + 1) * 512]) for c in range(8)]),
                     reads=WOUT, writes=[bk])
                p.op("act", copy_fn("act", asb[:, half * 512:(half + 1) * 512], bank[:, 0:512]), reads=[bk],
                     writes=[f"asb{half}"])
            c1 = col3(j, 0)
            p.op("act", lambda e: e.activation(out=junk[:], in_=asb[:], func=AF.Square, accum_out=ss[:, c1:c1 + 1]),
                 reads=["asb0", "asb1"], writes=[f"ss{c1}"])
            rstd_ops(c1, c1)
            p.op("dve", lambda e: e.scalar_tensor_tensor(out=asb[:], in0=asb[:], scalar=rstd[:, c1:c1 + 1],
                                                         in1=gpm[:], op0=ALU.mult, op1=ALU.mult),
                 reads=["asb0", "asb1", f"rstd{c1}", "gpm"], writes=["asb0", "asb1"])
            p.op("dve", lambda e: e.tensor_tensor(out=h1[hsel][:, tb, :], in0=asb[:], in1=xres[:], op=ALU.add),
                 reads=["asb0", "asb1", "xres"], writes=[f"h1:{hsel}:{tb}"])
            c2 = col3(j, 1)
            p.op("act", lambda e: e.activation(out=junk[:], in_=h1[hsel][:, tb, :], func=AF.Square,
                                               accum_out=ss[:, c2:c2 + 1]),
                 reads=[f"h1:{hsel}:{tb}"], writes=[f"ss{c2}"])
            rstd_ops(c2, c2)
            hb = cnt3["h2"] % 2
            cnt3["h2"] += 1
            pre_state[(G, tb)] = hb
            p.op("dve", lambda e: e.scalar_tensor_tensor(
                out=hn2_tok[hb][:], in0=h1[hsel][:, tb, :], scalar=rstd[:, c2:c2 + 1], in1=gpf[:],
                op0=ALU.mult, op1=ALU.mult),
                 reads=[f"h1:{hsel}:{tb}", f"rstd{c2}", "gpf"], writes=[f"hn2tok{hb}"])
        return f

    pre_state = {}

    def pre_b(G, tb):
        def f():
            hb = pre_state[(G, tb)]
            hb2 = G % 2
            transposes(hn2_tok[hb], [f"hn2tok{hb}"])
            p.op("act", copy_fn("act", hn2T[hb2][:, :, tb * 128:(tb + 1) * 128], Tv), reads=["T"],
                 writes=[f"hn2T{hb2}:{tb}"])
        return f

    def gu_step(G, f):
        def fn():
            hb2 = G % 2
            h2k = [f"hn2T{hb2}:{tb}" for tb in range(4)]
            wb = cnt3["wg"] % 3
            cnt3["wg"] += 1
            p.op("pool", dma(wgub[wb][:].rearrange("p c n -> p (c n)"), wgu_bf[f * 128:(f + 1) * 128, :]),
                 reads=["wgu_bf"], writes=[f"wgu{wb}"], lane=f"g{wb}")
            bg, bgk = next_bank()
            p.op("pe", mm_group(bg[:, 0:512], [(wgub[wb][:, c, 0:128], hn2T[hb2][:, c, :]) for c in range(8)]),
                 reads=h2k + [f"wgu{wb}"], writes=[bgk])
            bu, buk = next_bank()
            p.op("pe", mm_group(bu[:, 0:512], [(wgub[wb][:, c, 128:256], hn2T[hb2][:, c, :]) for c in range(8)]),
                 reads=h2k + [f"wgu{wb}"], writes=[buk])
            eb = cnt3["es"] % 2
            cnt3["es"] += 1
            p.op("act", lambda e: e.activation(out=esb[eb][:], in_=bg[:, 0:512], func=AF.Exp, scale=-1.0),
                 reads=[bgk], writes=[f"esb{eb}"])
            p.op("act", lambda e: e.activation(out=esb[eb][:], in_=esb[eb][:], func=AF.Ln, bias=1.0),
                 reads=[f"esb{eb}"], writes=[f"esb{eb}"])
            p.op("act", lambda e: e.activation(out=esb[eb][:], in_=esb[eb][:], func=AF.Exp, scale=-1.0),
                 reads=[f"esb{eb}"], writes=[f"esb{eb}"])
            p.op("dve", lambda e: e.tensor_tensor(out=esb[eb][:], in0=bg[:, 0:512], in1=esb[eb][:], op=ALU.mult),
                 reads=[bgk, f"esb{eb}"], writes=[f"esb{eb}"])
            p.op("dve", lambda e: e.tensor_tensor(out=actT[:, f, :], in0=bu[:, 0:512], in1=esb[eb][:], op=ALU.mult),
                 reads=[buk, f"esb{eb}"], writes=[f"actT{f}"])
        return fn

    def post(G, tb):
        def f():
            j = 4 * G + tb
            hsel = G % 2
            ak = [f"actT{f}" for f in range(NF)]
            for half in range(2):
                bank, bk = next_bank()
                p.op("pe", mm_group(bank[:, 0:512], [(actT[:, f, tb * 128:(tb + 1) * 128],
                                                      wdnb[:, f, half * 512:(half + 1) * 512]) for f in range(NF)]),
                     reads=ak + WDN, writes=[bk])
                p.op("act", copy_fn("act", fsb[:, half * 512:(half + 1) * 512], bank[:, 0:512]), reads=[bk],
                     writes=[f"asb{half}"])
            c3 = col3(j, 2)
            p.op("act", lambda e: e.activation(out=junk[:], in_=fsb[:], func=AF.Square, accum_out=ss[:, c3:c3 + 1]),
                 reads=["asb0", "asb1"], writes=[f"ss{c3}"])
            rstd_ops(c3, c3)
            p.op("dve", lambda e: e.scalar_tensor_tensor(out=fsb[:], in0=fsb[:], scalar=rstd[:, c3:c3 + 1],
                                                         in1=gqf[:], op0=ALU.mult, op1=ALU.mult),
                 reads=["asb0", "asb1", f"rstd{c3}", "gqf"], writes=["asb0", "asb1"])
            p.op("dve", lambda e: e.tensor_tensor(out=h1[hsel][:, tb, :], in0=fsb[:], in1=h1[hsel][:, tb, :],
                                                  op=ALU.add),
                 reads=["asb0", "asb1", f"h1:{hsel}:{tb}"], writes=[f"h1:{hsel}:{tb}"])
            p.op("sp", dma(out[j * 128:(j + 1) * 128, :], h1[hsel][:, tb, :]), reads=[f"h1:{hsel}:{tb}"],
                 lane=f"o{tb}")
        return f

    for stp in (pre_a(0, 0), pre_a(0, 1), pre_b(0, 0), pre_a(0, 2), pre_b(0, 1), pre_a(0, 3), pre_b(0, 2),
                pre_b(0, 3)):
        stp()
    A_AT = {1: 0, 6: 1, 11: 2, 16: 3}
    B_AT = {5: 0, 10: 1, 15: 2, 20: 3}
    for G in range(4):
        for f in range(NF):
            gu_step(G, f)()
            if G + 1 < 4:
                if f in A_AT:
                    pre_a(G + 1, A_AT[f])()
                if f in B_AT:
                    pre_b(G + 1, B_AT[f])()
        for tb in range(4):
            post(G, tb)()


    fin = [p.lanes[f"o{i}"][-1] for i in range(4)]
    p.emit(nc, final_waits=fin)
    return nc


def _t5_bucket(d):
    n = np.maximum(d, 0).astype(np.int64)
    nf = np.maximum(n, 1).astype(np.float32)
    large = 16 + (np.log(nf / np.float32(16)) / np.float32(np.log(128 / 16)) * np.float32(16)).astype(np.int32)
    large = np.minimum(large, 31)
    return np.where(n < 16, n, large)


_NC_CACHE = {}


def prep(x, meta_tokens, rel_bias, ln_pre_mix, ln_post_mix, ln_pre_ffn, ln_post_ffn,
         w_in, b_forget, sinks, w_out, w_gate_up, w_down):
    f32 = np.float32
    x = np.asarray(x, f32)
    tab = np.asarray(rel_bias, f32)
    B = x.shape[0]
    w_in0 = np.asarray(w_in, f32)[0]
    qa = w_in0[:, 0:512].reshape(D, 2, 4, 64).transpose(0, 2, 1, 3).reshape(D, 512)
    w_in_r = np.ascontiguousarray(np.concatenate(
        [qa, w_in0[:, 512:640], w_in0[:, 640:768], w_in0[:, 2304:2312], w_in0[:, 768:1280],
         w_in0[:, 1280:1792], w_in0[:, 1792:2304]], axis=1))
    w_out0 = np.ascontiguousarray(np.asarray(w_out, f32)[0])
    wgu0 = np.asarray(w_gate_up, f32)[0]
    gate = wgu0[:, :DFF].reshape(8, 128, NF, 128)
    up = wgu0[:, DFF:].reshape(8, 128, NF, 128)
    w_gu_r = np.ascontiguousarray(np.concatenate([gate, up], axis=3).transpose(2, 1, 0, 3).reshape(NF * 128, 2048))
    w_dn0 = np.ascontiguousarray(np.asarray(w_down, f32)[0])

    def bc(v, n=128):
        return np.ascontiguousarray(np.broadcast_to(np.asarray(v, f32).reshape(1, -1), (n, v.size)))

    g1, g2, g3, g4 = (bc(np.asarray(a, f32)[0]) for a in (ln_pre_mix, ln_post_mix, ln_pre_ffn, ln_post_ffn))
    bfb = np.ascontiguousarray(np.tile(bc(np.asarray(b_forget, f32)[0]), (1, 4)))
    sinkb = bc(np.asarray(sinks, f32)[0])
    ebmc = bc(tab[31])
    metap = np.zeros((128, D), f32)
    metap[:16] = np.asarray(meta_tokens, f32)

    k = np.arange(128)[:, None, None]
    kt = np.arange(2)[None, :, None]
    q = np.arange(128)[None, None, :]
    d = q + 128 - (kt * 128 + k)
    valid = (d >= 0) & (d < 128)
    bw = tab[_t5_bucket(d)]
    bw = np.where(valid[..., None], bw, f32(NEG)).transpose(0, 1, 3, 2)
    ebw = np.ascontiguousarray(bw.reshape(128, 2048).astype(f32))
    maskdiag = np.where(np.arange(128)[:, None] <= np.arange(128)[None, :], f32(1), f32(0)).astype(f32)
    ident = np.eye(128, dtype=f32)
    tri = np.triu(np.ones((128, 128), f32))
    ones = np.ones((128, 128), f32)
    sel63 = np.zeros((128, 128), f32)
    sel63[63, :] = 1.0

    in_maps = []
    for c in range(8):
        b, par = c // 2, c % 2
        if par == 1:
            xr = x[b]
        else:
            xr = np.concatenate([np.zeros((128, D), f32), x[b][:-128]], axis=0)
        mi = np.arange(16)[:, None]
        qq = np.arange(128)[None, :]
        dm = 16 + par * 128 + qq - mi
        bm0 = tab[_t5_bucket(dm)].transpose(0, 2, 1)
        ebm0 = np.zeros((128, 8, 128), f32)
        ebm0[:16] = bm0
        cst = np.zeros((128, 8), f32)
        cst[:, 0] = 1.0 if par == 1 else 0.0
        cst[:, 1] = 0.0 if par == 1 else NEG
        cst[:16, 2] = 1.0
        in_maps.append({
            "xr": np.ascontiguousarray(xr), "metap": metap, "w_in": w_in_r, "w_out": w_out0, "w_gu": w_gu_r,
            "w_dn": w_dn0, "g_pre_mix": g1, "g_post_mix": g2, "g_pre_ffn": g3, "g_post_ffn": g4,
            "bfb": bfb, "sinkb": sinkb, "ebw": ebw, "ebm0": np.ascontiguousarray(ebm0.reshape(128, 1024)),
            "ebmc": ebmc, "maskdiag": maskdiag, "ident": ident, "tri": tri, "ones": ones, "sel63": sel63,
            "cst": cst,
        })
    return in_maps


def kernel(x, meta_tokens, rel_bias, ln_pre_mix, ln_post_mix, ln_pre_ffn, ln_post_ffn,
           w_in, b_forget, sinks, w_out, w_gate_up, w_down):
    f32 = np.float32
    in_maps = prep(x, meta_tokens, rel_bias, ln_pre_mix, ln_post_mix, ln_pre_ffn, ln_post_ffn,
                   w_in, b_forget, sinks, w_out, w_gate_up, w_down)
    B = np.asarray(x).shape[0]
    nc = build_program()
    res = run_bass_kernel_spmd(nc, in_maps, core_ids=list(range(8)))
    outp = np.zeros((B, 4096, D), f32)
    for c in range(8):
        b, par = c // 2, c % 2
        o = np.asarray(res.results[c]["out"], f32).reshape(NS, 128, D)
        outp[b].reshape(32, 128, D)[par::2] = o
    return outp
```

```python
import contextlib
import numpy as np
import concourse.bass as bass
import concourse.mybir as mybir
from concourse.bass_utils import run_bass_kernel_spmd

F32 = mybir.dt.float32
BF16 = mybir.dt.bfloat16
AF = mybir.ActivationFunctionType
ALU = mybir.AluOpType

D = 1024
NT = 33
NS = 16
DFF = 2816
NF = 22
EPS = 1e-6
NEG = -30000.0

C_QA, C_KA, C_VA, C_F, C_QB, C_KB, C_VB = 0, 512, 640, 768, 776, 1288, 1800


class Op:
    __slots__ = ("eng", "fn", "deps", "lane", "token", "needed", "idx", "is_dma")

    def __init__(self, eng, fn, deps, lane=None):
        self.eng = eng
        self.fn = fn
        self.deps = deps
        self.lane = lane
        self.token = None
        self.needed = False
        self.is_dma = lane is not None


class Prog:
    ENGS = ("pe", "act", "dve", "pool", "sp")

    def __init__(self):
        self.ops = {e: [] for e in self.ENGS}
        self.last_writer = {}
        self.readers = {}
        self.barrier_ops = []
        self.lanes = {}
        self.n = 0
        self.limit = None

    @staticmethod
    def _sig(op):
        return op.lane if op.is_dma else op.eng

    def op(self, eng, fn, reads=(), writes=(), lane=None, extra=(), force=False):
        if self.limit is not None and self.n >= self.limit and not force:
            return None
        deps = {}

        def add(d):
            if d is None:
                return
            k = self._sig(d)
            cur = deps.get(k)
            if cur is None or cur.idx < d.idx:
                deps[k] = d

        for k in reads:
            add(self.last_writer.get(k))
        for k in writes:
            add(self.last_writer.get(k))
            for r in self.readers.get(k, {}).values():
                add(r)
        for d in extra:
            add(d)
        for d in self.barrier_ops:
            add(d)
        o = Op(eng, fn, list(deps.values()), lane=lane)
        o.idx = self.n
        self.n += 1
        self.ops[eng].append(o)
        if lane is not None:
            self.lanes.setdefault(lane, []).append(o)
        for d in o.deps:
            d.needed = True
        for k in reads:
            self.readers.setdefault(k, {})[self._sig(o)] = o
        for k in writes:
            self.last_writer[k] = o
            self.readers[k] = {}
        return o

    def barrier(self):
        b = []
        for e in self.ENGS:
            comp = [o for o in self.ops[e] if not o.is_dma]
            if comp:
                b.append(comp[-1])
        for lst in self.lanes.values():
            b.append(lst[-1])
        for o in b:
            o.needed = True
        self.barrier_ops = b

    def emit(self, nc, final_waits=()):
        for o in final_waits:
            o.needed = True
        for e in self.ENGS:
            c = 0
            for o in self.ops[e]:
                if not o.is_dma and o.needed:
                    c += 1
                    o.token = c
        for lst in self.lanes.values():
            c = 0
            for o in lst:
                c += 16
                o.token = c
        sem_names = list(self.ENGS[:4]) + sorted(self.lanes.keys())
        with contextlib.ExitStack() as st:
            sems = {n: st.enter_context(nc.semaphore("s_" + n)) for n in sem_names}
            block = st.enter_context(nc.Block())
            hw = {"pe": block.tensor, "act": block.scalar, "dve": block.vector,
                  "pool": block.gpsimd, "sp": block.sync}

            def make(ekey):
                def body(eng):
                    waited = {}

                    def wait(d):
                        sk = self._sig(d)
                        if waited.get(sk, 0) < d.token:
                            eng.wait_ge(sems[sk], d.token)
                            waited[sk] = d.token

                    for o in self.ops[ekey]:
                        for d in o.deps:
                            wait(d)
                        ins = o.fn(eng)
                        if o.is_dma:
                            ins.then_inc(sems[o.lane], 16)
                        elif o.needed:
                            ins.then_inc(sems[ekey], 1)
                    if ekey == "sp":
                        for d in final_waits:
                            wait(d)
                return body

            for ekey in self.ENGS:
                if self.ops[ekey] or ekey == "sp":
                    hw[ekey](make(ekey))


class Alloc:
    def __init__(self, nc, base, top, tag):
        self.nc, self.p, self.top, self.tag = nc, base, top, tag
        self.i = 0

    def a(self, name, shape, dt):
        esz = 2 if dt == BF16 else 4
        n = esz
        for s in shape[1:]:
            n *= s
        off = (self.p + 63) // 64 * 64
        assert off + n <= self.top, f"SBUF overflow in {self.tag}: {name} needs {off + n} > {self.top}"
        self.p = off + n
        self.i += 1
        return self.nc.alloc_sbuf_tensor_at(f"{self.tag}_{name}", list(shape), dt, offset=off)


def build_program(stop=None, dumps=()):
    nc = bass.Bass("TRN2", target_bir_lowering=False)
    DBG = {}

    def finish(loc):
        fin = []
        for i, name in enumerate(dumps):
            t = loc[name]
            shp = list(t.shape)
            flat = 1
            for v in shp[1:]:
                flat *= v
            dd = nc.dram_tensor("dbg_" + name, [shp[0], flat], t.dtype, kind="ExternalOutput").ap()
            src = t[:]
            if len(shp) == 3:
                src = src.rearrange("p a b -> p (a b)")
            elif len(shp) == 4:
                src = src.rearrange("p a b c -> p (a b c)")
            fin.append(p.op("sp", dma(dd, src), lane=f"dbg{i}", force=True))
        p.emit(nc, final_waits=fin)
        return nc


    def din(name, shape):
        return nc.dram_tensor(name, list(shape), F32, kind="ExternalInput").ap()

    xr = din("xr", [4096, D])
    metap = din("metap", [128, D])
    w_in = din("w_in", [D, 2312])
    w_out = din("w_out", [D, D])
    w_gu = din("w_gu", [NF * 128, 2048])
    w_dn = din("w_dn", [DFF, D])
    g_pre_mix = din("g_pre_mix", [128, D])
    g_post_mix = din("g_post_mix", [128, D])
    g_pre_ffn = din("g_pre_ffn", [128, D])
    g_post_ffn = din("g_post_ffn", [128, D])
    bfb_d = din("bfb", [128, 32])
    sinkb_d = din("sinkb", [128, 8])
    ebw_d = din("ebw", [128, 2048])
    ebm0_d = din("ebm0", [128, 1024])
    ebmc_d = din("ebmc", [128, 8])
    mask_d = din("maskdiag", [128, 128])
    ident_d = din("ident", [128, 128])
    tri_d = din("tri", [128, 128])
    ones_d = din("ones", [128, 128])
    sel_d = din("sel63", [128, 128])
    cst_d = din("cst", [128, 8])
    out = nc.dram_tensor("out", [NS * 128, D], F32, kind="ExternalOutput").ap()
    wgu_bf = nc.dram_tensor("wgu_bf", [NF * 128, 2048], BF16, kind="Internal").ap()

    p = Prog()
    import os
    if os.environ.get("KLIMIT"):
        p.limit = int(os.environ["KLIMIT"])
    BASE, TOP = 16512, 229344

    lo = Alloc(nc, BASE, TOP, "lo")
    identb = lo.a("identb", [128, 128], BF16)
    cst = lo.a("cst", [128, 8], F32)
    ss = lo.a("ss", [128, 64], F32)
    ms = lo.a("ms", [128, 64], F32)
    rstd = lo.a("rstd", [128, 64], F32)
    junk = lo.a("junk", [128, D], BF16)
    LO_END = lo.p

    MIXT_OFF = (TOP - 8 * 2048 * 2) // 64 * 64
    mixT = nc.alloc_sbuf_tensor_at("mixT", [128, 8, 2048], BF16, offset=MIXT_OFF)

    st = Alloc(nc, LO_END, MIXT_OFF, "st")
    KTf = st.a("KTf", [128, 4, NT * 128], BF16)
    QTf = st.a("QTf", [128, 4, NS * 128], BF16)
    Vf = st.a("Vf", [128, NT, 8, 65], BF16)
    KTs = st.a("KTs", [128, NT * 128], BF16)
    QTs = st.a("QTs", [128, 4, NS * 128], BF16)
    Vs = st.a("Vs", [128, NT, 2, 65], BF16)
    maskb = st.a("maskb", [128, 128], BF16)
    tri = st.a("tri", [128, 128], F32)
    ones = st.a("ones", [128, 128], F32)
    sel63 = st.a("sel63", [128, 128], F32)
    bfb = st.a("bfb", [128, 32], F32)
    esink = st.a("esink", [128, 8], F32)
    ebmc = st.a("ebmc", [128, 8], F32)
    fb_all = st.a("fb_all", [128, NT, 8], F32)
    e_all = st.a("e_all", [128, NT, 8], F32)
    L_all = st.a("L_all", [128, NT, 8], F32)
    tot_sb = st.a("tot_sb", [128, NT, 8], F32)
    incl = st.a("incl", [128, NT, 8], F32)
    cumL = st.a("cumL", [128, NT, 8], F32)
    cref = st.a("cref", [128, NS, 8], F32)
    ST_END = st.p

    r1 = Alloc(nc, ST_END, TOP, "r1")
    winb = r1.a("winb", [128, 8, 2312], BF16)
    xs = [r1.a(f"xs{i}", [128, D], F32) for i in range(3)]
    hn_tok = [r1.a(f"hntok{i}", [128, D], BF16) for i in range(2)]
    hnT = [r1.a(f"hnT{i}", [128, 8, 512], BF16) for i in range(2)]
    gpre = r1.a("gpre", [128, D], F32)

    r2 = Alloc(nc, ST_END, MIXT_OFF, "r2")
    PT = [r2.a(f"PT{i}", [128, 1024], BF16) for i in range(2)]
    Vp = [r2.a(f"Vp{i}", [128, NT, 65], BF16) for i in range(4)]
    aj = [r2.a(f"aj{i}", [128, NT, 8], F32) for i in range(2)]
    ebw = r2.a("ebw", [128, 2, 8, 128], F32)
    ebm0 = r2.a("ebm0", [128, 8, 128], F32)
    maskn2 = nc.alloc_sbuf_tensor_at("maskn2", [128, 256], BF16, offset=tri.manual_sbuf_range[0])
    PT.append(nc.alloc_sbuf_tensor_at("PT2", [128, 1024], BF16, offset=fb_all.manual_sbuf_range[0]))
    assert e_all.manual_sbuf_range[1] - fb_all.manual_sbuf_range[0] >= 2048
    _eo = ebm0.manual_sbuf_range[0]
    qpad = [nc.alloc_sbuf_tensor_at(f"qpad{i}", [128, 8, 128], BF16, offset=_eo + 2048 * i) for i in range(2)]
    Ef = r2.a("Ef", [128, 1024], F32)
    Efm = r2.a("Efm", [128, 512], F32)
    PTs = r2.a("PTs", [128, 1024], BF16)
    PTm = r2.a("PTm", [128, 512], BF16)
    mix_tok = [r2.a(f"mixtok{i}", [128, D], BF16) for i in range(2)]
    mix_tok.append(nc.alloc_sbuf_tensor_at("mixtok2", [128, D], BF16, offset=tot_sb.manual_sbuf_range[0]))
    assert incl.manual_sbuf_range[1] - tot_sb.manual_sbuf_range[0] >= 2048
    den = r2.a("den", [128, 8], F32)
    rec = r2.a("rec", [128, 8], F32)

    SWA_LO, SWA_HI = KTs.manual_sbuf_range[0], Vs.manual_sbuf_range[1]
    r3a = Alloc(nc, SWA_LO, SWA_HI, "r3a")
    woutb = r3a.a("woutb", [128, 8, D], BF16)
    gpm = r3a.a("gpm", [128, D], F32)
    gpf = r3a.a("gpf", [128, D], F32)
    gqf = r3a.a("gqf", [128, D], F32)
    xres = r3a.a("xres", [128, D], F32)
    rA = Alloc(nc, LO_END, SWA_LO, "rA")
    wdnb = rA.a("wdnb", [128, NF, D], BF16)
    h1 = [rA.a(f"h1_{i}", [128, 4, D], F32) for i in range(2)]
    asb = rA.a("asb", [128, D], F32)
    rB = Alloc(nc, (SWA_HI + 63) // 64 * 64, MIXT_OFF, "rB")
    actT = rB.a("actT", [128, NF, 512], BF16)
    hn2T = [rB.a(f"hn2T{i}", [128, 8, 512], BF16) for i in range(2)]
    wgub = [rB.a(f"wgub{i}", [128, 8, 256], BF16) for i in range(3)]
    esb = [rB.a(f"esb{i}", [128, 512], F32) for i in range(2)]
    hn2_tok = [rA.a("hn2tok0", [128, D], BF16), rB.a("hn2tok1", [128, D], BF16)]
    fsb = asb

    SA = nc.alloc_psum_tensor("SA", [128, 1024], F32)
    SB = nc.alloc_psum_tensor("SB", [128, 1024], F32)
    OA0 = nc.alloc_psum_tensor("OA0", [128, 512], F32)
    OA1 = nc.alloc_psum_tensor("OA1", [128, 512], F32)
    OB0 = nc.alloc_psum_tensor("OB0", [128, 512], F32)
    OB1 = nc.alloc_psum_tensor("OB1", [128, 512], F32)
    O0 = OA0
    T = OB1[:].bitcast(BF16)
    banks = [SA[:, 0:512], SA[:, 512:1024], SB[:, 0:512], SB[:, 512:1024], OA0[:], OA1[:], OB0[:]]
    OPAIR = [(OA0, OA1), (OB0, OB1)]
    PK = [("PA0", "PA1"), ("PB0", "PB1")]
    rot = [0]

    def next_bank():
        i = rot[0] % 7
        rot[0] += 1
        return banks[i], f"b{i}"

    evc = [0]

    def evac_eng():
        evc[0] += 1
        return "act" if evc[0] % 2 else "dve"

    def copy_fn(eng, dst, src):
        if eng == "act":
            return lambda e: e.activation(out=dst, in_=src, func=AF.Copy)
        return lambda e: e.tensor_copy(out=dst, in_=src)

    def dma(dst, src):
        return lambda e: e.dma_start(out=dst, in_=src)

    def mm_group(outap, pairs):
        def f(e):
            ins = None
            n = len(pairs)
            for i, (l, r) in enumerate(pairs):
                ins = e.matmul(outap, lhsT=l, rhs=r, start=(i == 0), stop=(i == n - 1))
            return ins
        return f

    def transposes(src_tok, keyreads, Tap=None, tkey="T"):
        Tap = T if Tap is None else Tap

        def f(e):
            ins = None
            for c in range(8):
                ins = e.transpose(Tap[:, c * 128:(c + 1) * 128], src_tok[:, c * 128:(c + 1) * 128], identb[:])
            return ins
        return p.op("pe", f, reads=list(keyreads) + ["identb"], writes=[tkey])

    Tv = T[:].rearrange("p (c r) -> p c r", r=128)

    def rstd_ops(col, key):
        p.op("act", lambda e: e.activation(out=ms[:, col:col + 1], in_=ss[:, col:col + 1], func=AF.Ln,
                                           scale=1.0 / D, bias=EPS),
             reads=[f"ss{key}"], writes=[f"ms{key}"])
        p.op("act", lambda e: e.activation(out=rstd[:, col:col + 1], in_=ms[:, col:col + 1], func=AF.Exp, scale=-0.5),
             reads=[f"ms{key}"], writes=[f"rstd{key}"])

    p.op("sp", dma(cst[:], cst_d), writes=["cst"], lane="c0")
    p.op("sp", dma(gpre[:], g_pre_mix), writes=["gpre"], lane="c1")
    p.op("sp", dma(bfb[:], bfb_d), writes=["bfb"], lane="c2")
    p.op("sp", dma(tri[:], tri_d), writes=["tri"], lane="c3")
    p.op("sp", dma(ones[:], ones_d), writes=["ones"], lane="c3")
    p.op("sp", dma(sel63[:], sel_d), writes=["sel63"], lane="c3")
    p.op("sp", dma(esink[:], sinkb_d), writes=["esink"], lane="c3")
    lastc3 = p.op("sp", dma(ebmc[:], ebmc_d), writes=["ebmc"], lane="c3")
    for k in ("tri", "ones", "sel63", "esink", "ebmc"):
        p.last_writer[k] = lastc3
    p.op("pool", dma(identb[:], ident_d), writes=["identb"], lane="c4")
    p.op("pool", dma(maskb[:], mask_d), writes=["maskb"], lane="c5")
    w_in_v = w_in.rearrange("(c p) n -> p c n", p=128)
    WGRP = {"kb": (C_KB, C_KB + 512), "kavaf": (C_KA, C_QB), "vb": (C_VB, C_VB + 512),
            "qb": (C_QB, C_QB + 512), "qa": (C_QA, C_QA + 512)}
    for gi_, (gname, (c0, c1)) in enumerate(WGRP.items()):
        p.op("pool", dma(winb[:, :, c0:c1], w_in_v[:, :, c0:c1]), writes=[f"win_{gname}"], lane=f"w{3 + gi_}")
    WIN = []
    conv_gate = []

    def emit_wgu_conversion():
        for f in range(NF):
            p.op("pool", dma(wgu_bf[f * 128:(f + 1) * 128, :], w_gu[f * 128:(f + 1) * 128, :]), writes=["wgu_bf"],
                 lane="w2", extra=conv_gate)
        p.last_writer["wgu_bf"] = p.lanes["w2"][-1]
    p.op("dve", lambda e: e.memset(Vf[:].rearrange("p t h d -> p (t h) d")[:, :, 64:65], 1.0), writes=["Vf1"])
    p.op("dve", lambda e: e.memset(Vs[:].rearrange("p t h d -> p (t h) d")[:, :, 64:65], 1.0), writes=["Vs1"])
    p.op("dve", lambda e: e.tensor_scalar(out=Vf[:, 0, :, 64:65], in0=Vf[:, 0, :, 64:65], scalar1=cst[:, 2:3],
                                          scalar2=None, op0=ALU.mult), reads=["Vf1", "cst"], writes=["Vf1"])
    p.op("dve", lambda e: e.tensor_scalar(out=Vs[:, 0, :, 64:65], in0=Vs[:, 0, :, 64:65], scalar1=cst[:, 2:3],
                                          scalar2=None, op0=ALU.mult), reads=["Vs1", "cst"], writes=["Vs1"])


    if stop == 'p0':
        p.barrier()
        return finish(locals())
    cnt = {"xs": 0, "hb": 0}

    def group_blocks(gi):
        if gi == 0:
            return [(0, None)]
        return [(4 * (gi - 1) + bi + 1, 4 * (gi - 1) + bi) for bi in range(4)]

    def A_steps(gi):
        blocks = group_blocks(gi)
        gb = gi % 2
        st8 = {}

        def front(bi):
            def f():
                t, xb = blocks[bi]
                s_ = cnt["xs"] % 3
                cnt["xs"] += 1
                hb = cnt["hb"] % 2
                cnt["hb"] += 1
                st8[bi] = hb
                src = metap if xb is None else xr[xb * 128:(xb + 1) * 128, :]
                p.op("sp", dma(xs[s_][:], src), writes=[f"xs{s_}"], lane=f"xs{s_}")
                p.op("act", lambda e: e.activation(out=junk[:], in_=xs[s_][:], func=AF.Square,
                                                   accum_out=ss[:, t:t + 1]),
                     reads=[f"xs{s_}"], writes=[f"ss{t}"])
                rstd_ops(t, t)
                p.op("dve", lambda e: e.scalar_tensor_tensor(
                    out=hn_tok[hb][:], in0=xs[s_][:], scalar=rstd[:, t:t + 1], in1=gpre[:],
                    op0=ALU.mult, op1=ALU.mult),
                     reads=[f"xs{s_}", f"rstd{t}", "gpre"], writes=[f"hntok{hb}"])
            return f

        def tr(bi):
            def f():
                hb = st8[bi]
                transposes(hn_tok[hb], [f"hntok{hb}"])
                p.op("act", copy_fn("act", hnT[gb][:, :, bi * 128:(bi + 1) * 128], Tv), reads=["T"],
                     writes=[f"hnT{gb}:{bi}"])
            return f
        n = len(blocks)
        if n == 1:
            return [front(0), tr(0)]
        return [front(0), front(1), tr(0), front(2), tr(1), front(3), tr(2), tr(3)]

    def B_steps(gi):
        blocks = group_blocks(gi)
        nb = len(blocks)
        N = nb * 128
        gb = gi % 2
        t0 = blocks[0][0]
        hk = [f"hnT{gb}:{bi}" for bi in range(nb)]
        steps = []

        def proj_T(col0, rhs_fn, ncols, dst, three=False, wk="kb"):
            def f():
                bank, bk = next_bank()
                pairs = [(winb[:, c, col0:col0 + 128], rhs_fn(c)) for c in range(8)]
                oap = bank[:, 0:ncols]
                if three:
                    oap = oap.rearrange("p (b r) -> p b r", r=128)
                o_ = p.op("pe", mm_group(oap, pairs), reads=hk + [f"win_{wk}"], writes=[bk])
                if gi == 2 and not conv_gate:
                    conv_gate.append(o_)
                    emit_wgu_conversion()
                eng = evac_eng()
                p.op(eng, copy_fn(eng, dst, bank[:, 0:ncols]), reads=[bk])
            return f

        full = lambda c: hnT[gb][:, c, 0:N]
        for i in range(4):
            steps.append(proj_T(C_KB + i * 128, full, N, KTf[:, i, t0 * 128:t0 * 128 + N]))
        steps.append(proj_T(C_KA, full, N, KTs[:, t0 * 128:t0 * 128 + N], wk="kavaf"))
        if gi >= 1:
            own = lambda c: hnT[gb][:, c, :].rearrange("p (b two r) -> p b two r", two=2, r=128)[:, :, 1, :]
            j0 = 2 * (gi - 1)
            for i in range(4):
                steps.append(proj_T(C_QB + i * 128, own, 256, QTf[:, i, j0 * 128:(j0 + 2) * 128], three=True, wk="qb"))
            for i in range(4):
                steps.append(proj_T(C_QA + i * 128, own, 256, QTs[:, i, j0 * 128:(j0 + 2) * 128], three=True, wk="qa"))

        def vproj(bi, t):
            lt = lambda c: hnT[gb][:, c, bi * 128:(bi + 1) * 128]

            def f1():
                bank, bk = next_bank()
                p.op("pe", mm_group(bank[:, 0:512], [(lt(c), winb[:, c, C_VB:C_VB + 512]) for c in range(8)]),
                     reads=hk + ["win_vb"], writes=[bk])
                p.op("act", copy_fn("act", Vf[:, t, :, 0:64], bank[:, 0:512].rearrange("p (h d) -> p h d", d=64)),
                     reads=[bk, "Vf1"])

            def f2():
                bank, bk = next_bank()
                p.op("pe", mm_group(bank[:, 0:136], [(lt(c), winb[:, c, C_VA:C_VA + 136]) for c in range(8)]),
                     reads=hk + ["win_kavaf"], writes=[bk])
                p.op("act", copy_fn("act", Vs[:, t, :, 0:64], bank[:, 0:128].rearrange("p (h d) -> p h d", d=64)),
                     reads=[bk, "Vs1"])
                p.op("dve", lambda e: e.tensor_tensor(out=fb_all[:, t, :], in0=bank[:, 128:136],
                                                      in1=bfb[:, 0:8], op=ALU.add),
                     reads=[bk, "bfb"], writes=[f"fb{t}"])
            return [f1, f2]
        for bi, (t, xb) in enumerate(blocks):
            steps.extend(vproj(bi, t))

        def lops():
            fk = [f"fb{t}" for t, _ in blocks]
            p.op("act", lambda e: e.activation(out=e_all[:, t0:t0 + nb, :], in_=fb_all[:, t0:t0 + nb, :],
                                               func=AF.Exp, scale=-1.0),
                 reads=fk, writes=[f"eL{gi}"])
            p.op("act", lambda e: e.activation(out=L_all[:, t0:t0 + nb, :], in_=e_all[:, t0:t0 + nb, :],
                                               func=AF.Ln, bias=1.0),
                 reads=[f"eL{gi}"], writes=[f"L{gi}"])
        steps.append(lops)
        return steps

    for stp in A_steps(0):
        stp()
    for gi in range(9):
        Bs = B_steps(gi)
        As = A_steps(gi + 1) if gi + 1 <= 8 else []
        nB, nA = len(Bs), len(As)
        ai = 0
        for bi_, b in enumerate(Bs):
            b()
            while ai < nA and (ai + 1) * nB <= (bi_ + 1) * nA:
                As[ai]()
                ai += 1
        while ai < nA:
            As[ai]()
            ai += 1

    if stop == 'p1':
        p.barrier()
        return finish(locals())
    p.barrier()
    p.op("sp", dma(ebw[:].rearrange("p a h q -> p (a h q)"), ebw_d), writes=["ebw"], lane="c1")
    p.op("sp", dma(ebm0[:].rearrange("p h q -> p (h q)"), ebm0_d), writes=["ebm0a", "ebm0b"], lane="c2")
    p.op("act", lambda e: e.activation(out=ebw[:].rearrange("p a h q -> p (a h q)"),
                                       in_=ebw[:].rearrange("p a h q -> p (a h q)"), func=AF.Exp),
         reads=["ebw"], writes=["ebw"])
    p.op("act", lambda e: e.activation(out=ebm0[:].rearrange("p h q -> p (h q)"),
                                       in_=ebm0[:].rearrange("p h q -> p (h q)"), func=AF.Exp),
         reads=["ebm0a", "ebm0b"], writes=["ebm0a", "ebm0b"])
    p.op("act", lambda e: e.activation(out=esink[:], in_=esink[:], func=AF.Exp), reads=["esink"], writes=["esink"])
    p.op("act", lambda e: e.activation(out=ebmc[:], in_=ebmc[:], func=AF.Exp), reads=["ebmc"], writes=["ebmc"])

    p.op("dve", lambda e: e.tensor_scalar(out=L_all[:, 0, :], in0=L_all[:, 0, :], scalar1=cst[:, 2:3], scalar2=None,
                                          op0=ALU.mult), reads=["cst"], writes=["L"])
    p.op("dve", lambda e: e.tensor_scalar(out=L_all[:, 1, :], in0=L_all[:, 1, :], scalar1=cst[:, 0:1], scalar2=None,
                                          op0=ALU.mult), reads=["cst", "L"], writes=["L"])
    Lflat = L_all[:].rearrange("p t h -> p (t h)")
    p.op("pe", lambda e: e.matmul(SA[:, 0:NT * 8], lhsT=tri[:], rhs=Lflat, start=True, stop=True),
         reads=["L", "tri"], writes=["SA"])
    p.op("pe", lambda e: e.matmul(SB[:, 0:NT * 8], lhsT=ones[:], rhs=Lflat, start=True, stop=True),
         reads=["L", "ones"], writes=["SB"])
    p.op("dve", lambda e: e.tensor_copy(out=tot_sb[:].rearrange("p t h -> p (t h)"), in_=SB[:, 0:NT * 8]),
         reads=["SB"], writes=["tot"])
    p.op("dve", lambda e: e.memset(aj[0][:], 1.0), writes=["aj0"])
    for h in range(8):
        p.op("dve", lambda e, h=h: e.tensor_tensor_scan(out=incl[:, :, h], data0=aj[0][:, :, h], data1=tot_sb[:, :, h],
                                                        initial=0.0, op0=ALU.mult, op1=ALU.add),
             reads=["tot", "aj0"], writes=[f"incl{h}"])
    p.op("dve", lambda e: e.tensor_tensor(out=cumL[:].rearrange("p t h -> p (t h)"), in0=SA[:, 0:NT * 8],
                                          in1=incl[:].rearrange("p t h -> p (t h)"), op=ALU.add),
         reads=["SA"] + [f"incl{h}" for h in range(8)], writes=["cumL"])
    p.op("dve", lambda e: e.tensor_tensor(out=cumL[:], in0=cumL[:], in1=tot_sb[:], op=ALU.subtract),
         reads=["cumL", "tot"], writes=["cumL"])
    p.op("dve", lambda e: e.tensor_scalar(out=cumL[:, 1, :], in0=cumL[:, 1, :], scalar1=cst[:, 1:2], scalar2=None,
                                          op0=ALU.add), reads=["cumL", "cst"], writes=["cumL"])
    cum_own = cumL[:, 1:NT, :].rearrange("p (j two) h -> p j two h", two=2)[:, :, 1, :]
    p.op("pe", lambda e: e.matmul(O0[:, 0:128].rearrange("p (j h) -> p j h", h=8), lhsT=sel63[:], rhs=cum_own,
                                  start=True, stop=True),
         reads=["cumL", "sel63"], writes=["PA0"])
    p.op("dve", lambda e: e.tensor_copy(out=cref[:].rearrange("p j h -> p (j h)"), in_=O0[:, 0:128]),
         reads=["PA0"], writes=["cref"])

    if stop == 'p15':
        p.barrier()
        return finish(locals())
    def Oreg_f(j, h):
        return OPAIR[j % 2][h % 2][:, (h // 2) * 65:(h // 2 + 1) * 65]

    pending_tr = [None]
    sbuf_i = [0]
    pt_i = [0]
    vp_i = [0]
    vp_of = {}
    swa_last_pe = {}
    WOUT = [f"wout{c}" for c in range(8)]

    def prefetch_phase3():
        dep = [swa_last_pe[NS - 1]]
        stg = [gpm, gpf, gqf]
        for c in range(8):
            sb_ = stg[c % 3]
            p.op("sp", dma(sb_[:], w_out[c * 128:(c + 1) * 128, :]), writes=[f"stg{c % 3}"], lane=f"c{c % 3}",
                 extra=dep)
            p.op("dve", lambda e, sb_=sb_, c=c: e.tensor_copy(out=woutb[:, c, :], in_=sb_[:]),
                 reads=[f"stg{c % 3}"], writes=[f"wout{c}"])
        p.op("sp", dma(gpm[:], g_post_mix), writes=["stg0", "gpm"], lane="c0")
        p.op("sp", dma(gpf[:], g_pre_ffn), writes=["stg1", "gpf"], lane="c1")
        p.op("sp", dma(gqf[:], g_post_ffn), writes=["stg2", "gqf"], lane="c2")

    def next_S():
        sb = sbuf_i[0] % 2
        sbuf_i[0] += 1
        return (SA, "SA") if sb == 0 else (SB, "SB")

    def a_step_dve(j):
        nt = 2 * j + 3
        ab = j % 2
        p.op("dve", lambda e: e.tensor_tensor(
            out=aj[ab][:, 0:nt, :], in0=cumL[:, 0:nt, :],
            in1=cref[:, j, :].unsqueeze(1).broadcast_to([128, nt, 8]), op=ALU.subtract),
             reads=["cumL", "cref"], writes=[f"aj{ab}"])

    def a_step_act(j):
        nt = 2 * j + 3
        ab = j % 2
        p.op("act", lambda e: e.activation(out=aj[ab][:, 0:nt, :], in_=aj[ab][:, 0:nt, :], func=AF.Exp),
             reads=[f"aj{ab}"], writes=[f"aj{ab}"])

    def a_step(j):
        a_step_dve(j)
        a_step_act(j)

    VCH = 8

    def ensure_vp(j, h):
        if (j, h) in vp_of or j >= NS:
            return
        nt = 2 * j + 3
        ab = j % 2
        vb = 2 * ((4 * j + h // 2) % 2) + (h % 2)
        vp_of[(j, h)] = vb
        eng = "dve" if h % 2 == 0 else "pool"
        for ck, (ta, tb_) in (("a", (0, min(VCH, nt))), ("b", (VCH, nt))):
            if tb_ <= ta:
                continue
            n_ = tb_ - ta
            p.op(eng, lambda e, ta=ta, tb_=tb_, n_=n_: e.tensor_tensor(
                out=Vp[vb][:, ta:tb_, :], in0=Vf[:, ta:tb_, h, :],
                in1=aj[ab][:, ta:tb_, h].unsqueeze(2).broadcast_to([128, n_, 65]), op=ALU.mult),
                 reads=[f"aj{ab}", "Vf1"], writes=[f"Vp{vb}:{ck}"])

    def qpad_step(j):
        qb = j % 2
        qv = qpad[qb][:].rearrange("p (i two) q -> p i two q", two=2)
        p.op("dve", lambda e: e.tensor_copy(out=qv[0:64, :, 0, :], in_=QTf[0:64, :, j * 128:(j + 1) * 128]),
             writes=[f"qpad{qb}"])
        p.op("dve", lambda e: e.tensor_copy(out=qv[64:128, :, 1, :], in_=QTf[64:128, :, j * 128:(j + 1) * 128]),
             writes=[f"qpad{qb}"])

    def swa_steps(j):
        mb = j % 3
        pidx = 1 if j == 0 else j % 2
        tq = 2 * j + 2
        tp = 2 * j + 1
        steps = []
        v3 = lambda a: a.rearrange("p (i q) -> p i q", q=128)
        for g in range(2):
            gs = slice(g * 64, (g + 1) * 64)
            qrhs = QTs[gs, :, j * 128:(j + 1) * 128]

            def sA(g=g, gs=gs, qrhs=qrhs):
                Sb, sk = next_S()

                def f(e):
                    e.matmul(v3(Sb[:, 0:512]), lhsT=KTs[gs, tp * 128:(tp + 1) * 128], rhs=qrhs, start=True, stop=True)
                    return e.matmul(v3(Sb[:, 512:1024]), lhsT=KTs[gs, tq * 128:(tq + 1) * 128], rhs=qrhs,
                                    start=True, stop=True)
                p.op("pe", f, writes=[sk])
                p.op("act", lambda e: e.activation(out=Ef[:], in_=Sb[:], func=AF.Exp, scale=0.125),
                     reads=[sk], writes=["Ef"])
                p.op("dve", lambda e: e.tensor_tensor(
                    out=PTs[:].rearrange("p (a i q) -> p a i q", a=2, i=4),
                    in0=Ef[:].rearrange("p (a i q) -> p a i q", a=2, i=4),
                    in1=ebw[:, :, g * 4:(g + 1) * 4, :], op=ALU.mult),
                     reads=["Ef", "ebw"], writes=["PTs"])
                if j == 0:
                    p.op("dve", lambda e: e.tensor_scalar(out=PTs[:, 0:512], in0=PTs[:, 0:512], scalar1=cst[:, 0:1],
                                                          scalar2=None, op0=ALU.mult),
                         reads=["PTs", "cst"], writes=["PTs"])

            def sB(g=g, gs=gs, qrhs=qrhs):
                Sb, sk = next_S()
                p.op("pe", lambda e: e.matmul(v3(Sb[:, 0:512]), lhsT=KTs[gs, 0:128], rhs=qrhs, start=True, stop=True),
                     writes=[sk])
                p.op("act", lambda e: e.activation(out=Efm[:], in_=Sb[:, 0:512], func=AF.Exp, scale=0.125),
                     reads=[sk], writes=["Efm"])
                if j == 0:
                    in1 = ebm0[:, g * 4:(g + 1) * 4, :]
                    rk = "ebm0a" if g == 0 else "ebm0b"
                else:
                    in1 = ebmc[:, g * 4:(g + 1) * 4].unsqueeze(2).broadcast_to([128, 4, 128])
                    rk = "ebmc"
                p.op("dve", lambda e: e.tensor_tensor(
                    out=PTm[:].rearrange("p (i q) -> p i q", i=4),
                    in0=Efm[:].rearrange("p (i q) -> p i q", i=4), in1=in1, op=ALU.mult),
                     reads=["Efm", rk], writes=["PTm"])

            def sPV(g=g):
                def f(e):
                    ins = None
                    for i in range(4):
                        o = OPAIR[pidx][0][:, i * 65:(i + 1) * 65]
                        e.matmul(o, lhsT=PTs[:, i * 128:(i + 1) * 128], rhs=Vs[:, tp, g, :], start=True, stop=False)
                        e.matmul(o, lhsT=PTs[:, 512 + i * 128:512 + (i + 1) * 128], rhs=Vs[:, tq, g, :],
                                 start=False, stop=False)
                        ins = e.matmul(o, lhsT=PTm[:, i * 128:(i + 1) * 128], rhs=Vs[:, 0, g, :],
                                       start=False, stop=True)
                    return ins
                xk = PK[pidx][0]
                swa_last_pe[j] = p.op("pe", f, reads=["PTs", "PTm", "Vs1"], writes=[xk])
                Ov = OPAIR[pidx][0][:, 0:260].rearrange("p (i d) -> p i d", d=65)
                p.op("dve", lambda e: e.tensor_tensor(out=den[:, 0:4], in0=Ov[:, :, 64],
                                                      in1=esink[:, g * 4:(g + 1) * 4], op=ALU.add),
                     reads=[xk, "esink"], writes=["den"])
                p.op("dve", lambda e: e.reciprocal(out=rec[:, 0:4], in_=den[:, 0:4]), reads=["den"], writes=["rec"])
                p.op("dve", lambda e: e.tensor_tensor(
                    out=mix_tok[mb][:, g * 256:(g + 1) * 256].rearrange("p (i d) -> p i d", d=64),
                    in0=Ov[:, :, 0:64], in1=rec[:, 0:4].unsqueeze(2).broadcast_to([128, 4, 64]), op=ALU.mult),
                     reads=[xk, "rec"], writes=[f"mixtok{mb}:s{g}"])
            steps += [sA, sB, sPV]
        return steps

    SL = []
    for j in range(NS):
        nt = 2 * j + 3
        ptiles = [(i, t) for i in range(4) for t in range(nt)]
        batches = [ptiles[x:x + 4] for x in range(0, len(ptiles), 4)]
        SL.append({"nt": nt, "tq": 2 * j + 2, "batches": batches, "nbt": len(batches), "touched": set(), "ei": 0,
                   "extras": [], "late": None})
    stream = [(j, b) for j in range(NS) for b in range(SL[j]["nbt"])]
    a_done = {0}
    pref_hi = [1]
    pend_vp = []
    tr_due = {}
    pb_of = {}

    def try_prefetch():
        for (jt, it) in list(pend_vp):
            if jt >= NS:
                pend_vp.remove((jt, it))
            elif jt in a_done:
                ensure_vp(jt, 2 * it)
                ensure_vp(jt, 2 * it + 1)
                pend_vp.remove((jt, it))

    def emit_pv(g):
        j, b = stream[g]
        sl = SL[j]
        batch = sl["batches"][b]
        nt = sl["nt"]
        pb = pb_of[g]
        hs = sorted(set(h for i, _ in batch for h in (2 * i, 2 * i + 1)))
        for h in hs:
            ensure_vp(j, h)

        def f(e):
            ins = None
            for u, (i, t) in enumerate(batch):
                for par in range(2):
                    h = 2 * i + par
                    ins = e.matmul(Oreg_f(j, h), lhsT=PT[pb][:, u * 256 + par * 128:u * 256 + (par + 1) * 128],
                                   rhs=Vp[vp_of[(j, h)]][:, t, :], start=(t == 0), stop=(t == nt - 1))
            return ins
        wk = [f"Of{j % 2}:{h}" for h in hs]
        for bk_ in (0, 1):
            if bk_ not in sl["touched"]:
                sl["touched"].add(bk_)
                wk.append(PK[j % 2][bk_])
        vkeys = sorted(set(f"Vp{vp_of[(j, 2 * i + par)]}:{'a' if t < VCH else 'b'}"
                           for i, t in batch for par in range(2)))
        p.op("pe", f, reads=[f"PT{pb}"] + vkeys, writes=wk)
        if b == sl["nbt"] - 1:
            finish_slot(j, g)

    def finish_slot(j, g):
        mb = j % 3
        for par in range(2):
            Ov = OPAIR[j % 2][par][:, 0:260].rearrange("p (i d) -> p i d", d=65)
            okeys = [f"Of{j % 2}:{h}" for h in range(par, 8, 2)]
            p.op("dve", lambda e, Ov=Ov: e.reciprocal(out=rec[:, 4:8], in_=Ov[:, :, 64]),
                 reads=okeys + [PK[j % 2][par]], writes=["rec2"])
            p.op("dve", lambda e, Ov=Ov, par=par: e.tensor_tensor(
                out=mix_tok[mb][:, 512:1024].rearrange("p (i two d) -> p i two d", two=2, d=64)[:, :, par, :],
                in0=Ov[:, :, 0:64], in1=rec[:, 4:8].unsqueeze(2).broadcast_to([128, 4, 64]), op=ALU.mult),
                 reads=okeys + ["rec2", PK[j % 2][par]], writes=[f"mixtok{mb}:f{par}"])

        def do_tr():
            Tj = OPAIR[j % 2][1][:].bitcast(BF16)
            tk = PK[j % 2][1]
            transposes(mix_tok[mb], [f"mixtok{mb}:s0", f"mixtok{mb}:s1", f"mixtok{mb}:f0", f"mixtok{mb}:f1"],
                       Tap=Tj, tkey=tk)

            def ev(e):
                ins = None
                for c in range(8):
                    ins = e.tensor_copy(out=mixT[:, c, j * 128:(j + 1) * 128], in_=Tj[:, c * 128:(c + 1) * 128])
                return ins
            if j < 6:
                p.op("act", copy_fn("act", mixT[:, :, j * 128:(j + 1) * 128],
                                    Tj.rearrange("p (c r) -> p c r", r=128)), reads=[tk])
            else:
                p.op("dve", ev, reads=[tk])
        tr_due[g + 4] = do_tr

    a_step(0)
    for stp in swa_steps(0):
        stp()
    for a_ in range(2):
        p.op("dve", lambda e, a_=a_: e.tensor_scalar(out=maskn2[:, a_ * 128:(a_ + 1) * 128], in0=maskb[:], scalar1=-1.0,
                                                     scalar2=240000.0, op0=ALU.add, op1=ALU.mult),
             reads=["maskb"], writes=["tri", "maskn2"])
    for qb_ in range(2):
        p.op("pool", lambda e, qb_=qb_: e.memset(qpad[qb_][:], 0.0),
             writes=["ebm0a" if qb_ == 0 else "ebm0b", f"qpad{qb_}"])
    qpad_step(0)
    for j in range(NS):
        if j + 1 < NS:
            def first_extra(j=j):
                a_step_dve(j + 1)
                qpad_step(j + 1)

            def second_extra(j=j):
                a_step_act(j + 1)
                a_done.add(j + 1)
            sw_ = swa_steps(j + 1)
            SL[j]["extras"] = [first_extra, sw_[0], sw_[1], second_extra] + sw_[2:]
    SL[NS - 1]["late"] = prefetch_phase3
    ensure_vp(0, 0)
    ensure_vp(0, 1)
    pend_vp.append((0, 1))

    G = len(stream)
    for g in range(G):
        j, b = stream[g]
        sl = SL[j]
        batch = sl["batches"][b]
        tq = sl["tq"]
        Sbuf, skey = next_S()

        def fs(e, batch=batch, Sbuf=Sbuf, j=j, tq=tq):
            ins = None
            for u, (i, t) in enumerate(batch):
                diag = (t == tq)
                ins = e.matmul(Sbuf[:, u * 256:(u + 1) * 256].rearrange("p (a q) -> p a q", a=2),
                               lhsT=KTf[:, i, t * 128:(t + 1) * 128],
                               rhs=qpad[j % 2][:, 2 * i:2 * i + 2, :], start=True, stop=not diag)
                if diag:
                    ins = e.matmul(Sbuf[:, u * 256:(u + 1) * 256], lhsT=identb[:], rhs=maskn2[:],
                                   start=False, stop=True)
            return ins
        p.op("pe", fs, reads=[f"qpad{j % 2}", "maskn2", "identb"], writes=[skey])
        if g in tr_due:
            tr_due.pop(g)()
        if g >= 2:
            emit_pv(g - 2)
        pb = g % 3
        pb_of[g] = pb
        n4 = len(batch)
        p.op("act", lambda e, Sbuf=Sbuf, pb=pb, n4=n4: e.activation(
            out=PT[pb][:, 0:n4 * 256], in_=Sbuf[:, 0:n4 * 256], func=AF.Exp, scale=0.125),
             reads=[skey], writes=[f"PT{pb}"])
        if g >= 1:
            j1, b1 = stream[g - 1]
            Pm = 4 * j1 + min(i for i, _ in SL[j1]["batches"][b1])
            while pref_hi[0] < Pm + 1:
                pref_hi[0] += 1
                pend_vp.append(divmod(pref_hi[0], 4))
        try_prefetch()
        if sl["late"] is not None and min(i for i, _ in batch) == 3:
            sl["late"]()
            sl["late"] = None
        ex = sl["extras"]
        while sl["ei"] < len(ex) and (sl["ei"] + 1) * sl["nbt"] <= (b + 1) * (len(ex) + 1):
            ex[sl["ei"]]()
            sl["ei"] += 1
        if b == sl["nbt"] - 1:
            while sl["ei"] < len(ex):
                ex[sl["ei"]]()
                sl["ei"] += 1
            try_prefetch()
    emit_pv(G - 2)
    emit_pv(G - 1)
    for g_ in sorted(tr_due):
        tr_due[g_]()
    tr_due.clear()

    p.barrier()
    if stop == 'p2':
        return finish(locals())
    for f in range(NF):
        p.op("pool", dma(wdnb[:, f, :], w_dn[f * 128:(f + 1) * 128, :]), writes=[f"wdn{f}"], lane="w1")
    lastw = p.lanes["w1"][-1]
    WDN = [f"wdn{f}" for f in range(NF)]
    for k in WDN:
        p.last_writer[k] = lastw


    cnt3 = {"xr": 0, "h2": 0, "wg": 0, "es": 0}

    def col3(j, k):
        return 33 + (3 * j) % 30 + k

    def pre_a(G, tb):
        def f():
            j = 4 * G + tb
            hsel = G % 2
            xb = 2 * j + 1
            p.op("sp", dma(xres[:], xr[xb * 128:(xb + 1) * 128, :]), writes=["xres"], lane="xs0")
            for half in range(2):
                bank, bk = next_bank()
                p.op("pe", mm_group(bank[:, 0:512], [(mixT[:, c, j * 128:(j + 1) * 128],
                                                      woutb[:, c, half * 512:(half + 1) * 512]) for c in range(8)]),
                     reads=WOUT, writes=[bk])
                p.op("act", copy_fn("act", asb[:, half * 512:(half + 1) * 512], bank[:, 0:512]), reads=[bk],
                     writes=[f"asb{half}"])
            c1 = col3(j, 0)
            p.op("act", lambda e: e.activation(out=junk[:], in_=asb[:], func=AF.Square, accum_out=ss[:, c1:c1 + 1]),
                 reads=["asb0", "asb1"], writes=[f"ss{c1}"])
            rstd_ops(c1, c1)
            p.op("dve", lambda e: e.scalar_tensor_tensor(out=asb[:], in0=asb[:], scalar=rstd[:, c1:c1 + 1],
                                                         in1=gpm[:], op0=ALU.mult, op1=ALU.mult),
                 reads=["asb0", "asb1", f"rstd{c1}", "gpm"], writes=["asb0", "asb1"])
            p.op("dve", lambda e: e.tensor_tensor(out=h1[hsel][:, tb, :], in0=asb[:], in1=xres[:], op=ALU.add),
                 reads=["asb0", "asb1", "xres"], writes=[f"h1:{hsel}:{tb}"])
            c2 = col3(j, 1)
            p.op("act", lambda e: e.activation(out=junk[:], in_=h1[hsel][:, tb, :], func=AF.Square,
                                               accum_out=ss[:, c2:c2 + 1]),
                 reads=[f"h1:{hsel}:{tb}"], writes=[f"ss{c2}"])
            rstd_ops(c2, c2)
            hb = cnt3["h2"] % 2
            cnt3["h2"] += 1
            pre_state[(G, tb)] = hb
            p.op("dve", lambda e: e.scalar_tensor_tensor(
                out=hn2_tok[hb][:], in0=h1[hsel][:, tb, :], scalar=rstd[:, c2:c2 + 1], in1=gpf[:],
                op0=ALU.mult, op1=ALU.mult),
                 reads=[f"h1:{hsel}:{tb}", f"rstd{c2}", "gpf"], writes=[f"hn2tok{hb}"])
        return f

    pre_state = {}

    def pre_b(G, tb):
        def f():
            hb = pre_state[(G, tb)]
            hb2 = G % 2
            transposes(hn2_tok[hb], [f"hn2tok{hb}"])
            p.op("act", copy_fn("act", hn2T[hb2][:, :, tb * 128:(tb + 1) * 128], Tv), reads=["T"],
                 writes=[f"hn2T{hb2}:{tb}"])
        return f

    def gu_step(G, f):
        def fn():
            hb2 = G % 2
            h2k = [f"hn2T{hb2}:{tb}" for tb in range(4)]
            wb = cnt3["wg"] % 3
            cnt3["wg"] += 1
            p.op("pool", dma(wgub[wb][:].rearrange("p c n -> p (c n)"), wgu_bf[f * 128:(f + 1) * 128, :]),
                 reads=["wgu_bf"], writes=[f"wgu{wb}"], lane=f"g{wb}")
            bg, bgk = next_bank()
            p.op("pe", mm_group(bg[:, 0:512], [(wgub[wb][:, c, 0:128], hn2T[hb2][:, c, :]) for c in range(8)]),
                 reads=h2k + [f"wgu{wb}"], writes=[bgk])
            bu, buk = next_bank()
            p.op("pe", mm_group(bu[:, 0:512], [(wgub[wb][:, c, 128:256], hn2T[hb2][:, c, :]) for c in range(8)]),
                 reads=h2k + [f"wgu{wb}"], writes=[buk])
            eb = cnt3["es"] % 2
            cnt3["es"] += 1
            p.op("act", lambda e: e.activation(out=esb[eb][:], in_=bg[:, 0:512], func=AF.Exp, scale=-1.0),
                 reads=[bgk], writes=[f"esb{eb}"])
            p.op("act", lambda e: e.activation(out=esb[eb][:], in_=esb[eb][:], func=AF.Ln, bias=1.0),
                 reads=[f"esb{eb}"], writes=[f"esb{eb}"])
            p.op("act", lambda e: e.activation(out=esb[eb][:], in_=esb[eb][:], func=AF.Exp, scale=-1.0),
                 reads=[f"esb{eb}"], writes=[f"esb{eb}"])
            p.op("dve", lambda e: e.tensor_tensor(out=esb[eb][:], in0=bg[:, 0:512], in1=esb[eb][:], op=ALU.mult),
                 reads=[bgk, f"esb{eb}"], writes=[f"esb{eb}"])
            p.op("dve", lambda e: e.tensor_tensor(out=actT[:, f, :], in0=bu[:, 0:512], in1=esb[eb][:], op=ALU.mult),
                 reads=[buk, f"esb{eb}"], writes=[f"actT{f}"])
        return fn

    def post(G, tb):
        def f():
            j = 4 * G + tb
            hsel = G % 2
            ak = [f"actT{f}" for f in range(NF)]
            for half in range(2):
                bank, bk = next_bank()
                p.op("pe", mm_group(bank[:, 0:512], [(actT[:, f, tb * 128:(tb + 1) * 128],
                                                      wdnb[:, f, half * 512:(half + 1) * 512]) for f in range(NF)]),
                     reads=ak + WDN, writes=[bk])
                p.op("act", copy_fn("act", fsb[:, half * 512:(half + 1) * 512], bank[:, 0:512]), reads=[bk],
                     writes=[f"asb{half}"])
            c3 = col3(j, 2)
            p.op("act", lambda e: e.activation(out=junk[:], in_=fsb[:], func=AF.Square, accum_out=ss[:, c3:c3 + 1]),
                 reads=["asb0", "asb1"], writes=[f"ss{c3}"])
            rstd_ops(c3, c3)
            p.op("dve", lambda e: e.scalar_tensor_tensor(out=fsb[:], in0=fsb[:], scalar=rstd[:, c3:c3 + 1],
                                                         in1=gqf[:], op0=ALU.mult, op1=ALU.mult),
                 reads=["asb0", "asb1", f"rstd{c3}", "gqf"], writes=["asb0", "asb1"])
            p.op("dve", lambda e: e.tensor_tensor(out=h1[hsel][:, tb, :], in0=fsb[:], in1=h1[hsel][:, tb, :],
                                                  op=ALU.add),
                 reads=["asb0", "asb1", f"h1:{hsel}:{tb}"], writes=[f"h1:{hsel}:{tb}"])
            p.op("sp", dma(out[j * 128:(j + 1) * 128, :], h1[hsel][:, tb, :]), reads=[f"h1:{hsel}:{tb}"],
                 lane=f"o{tb}")
        return f

    for stp in (pre_a(0, 0), pre_a(0, 1), pre_b(0, 0), pre_a(0, 2), pre_b(0, 1), pre_a(0, 3), pre_b(0, 2),
                pre_b(0, 3)):
        stp()
    A_AT = {1: 0, 6: 1, 11: 2, 16: 3}
    B_AT = {5: 0, 10: 1, 15: 2, 20: 3}
    for G in range(4):
        for f in range(NF):
            gu_step(G, f)()
            if G + 1 < 4:
                if f in A_AT:
                    pre_a(G + 1, A_AT[f])()
                if f in B_AT:
                    pre_b(G + 1, B_AT[f])()
        for tb in range(4):
            post(G, tb)()


    fin = [p.lanes[f"o{i}"][-1] for i in range(4)]
    p.emit(nc, final_waits=fin)
    return nc


def _t5_bucket(d):
    n = np.maximum(d, 0).astype(np.int64)
    nf = np.maximum(n, 1).astype(np.float32)
    large = 16 + (np.log(nf / np.float32(16)) / np.float32(np.log(128 / 16)) * np.float32(16)).astype(np.int32)
    large = np.minimum(large, 31)
    return np.where(n < 16, n, large)


_NC_CACHE = {}


def prep(x, meta_tokens, rel_bias, ln_pre_mix, ln_post_mix, ln_pre_ffn, ln_post_ffn,
         w_in, b_forget, sinks, w_out, w_gate_up, w_down):
    f32 = np.float32
    x = np.asarray(x, f32)
    tab = np.asarray(rel_bias, f32)
    B = x.shape[0]
    w_in0 = np.asarray(w_in, f32)[0]
    qa = w_in0[:, 0:512].reshape(D, 2, 4, 64).transpose(0, 2, 1, 3).reshape(D, 512)
    w_in_r = np.ascontiguousarray(np.concatenate(
        [qa, w_in0[:, 512:640], w_in0[:, 640:768], w_in0[:, 2304:2312], w_in0[:, 768:1280],
         w_in0[:, 1280:1792], w_in0[:, 1792:2304]], axis=1))
    w_out0 = np.ascontiguousarray(np.asarray(w_out, f32)[0])
    wgu0 = np.asarray(w_gate_up, f32)[0]
    gate = wgu0[:, :DFF].reshape(8, 128, NF, 128)
    up = wgu0[:, DFF:].reshape(8, 128, NF, 128)
    w_gu_r = np.ascontiguousarray(np.concatenate([gate, up], axis=3).transpose(2, 1, 0, 3).reshape(NF * 128, 2048))
    w_dn0 = np.ascontiguousarray(np.asarray(w_down, f32)[0])

    def bc(v, n=128):
        return np.ascontiguousarray(np.broadcast_to(np.asarray(v, f32).reshape(1, -1), (n, v.size)))

    g1, g2, g3, g4 = (bc(np.asarray(a, f32)[0]) for a in (ln_pre_mix, ln_post_mix, ln_pre_ffn, ln_post_ffn))
    bfb = np.ascontiguousarray(np.tile(bc(np.asarray(b_forget, f32)[0]), (1, 4)))
    sinkb = bc(np.asarray(sinks, f32)[0])
    ebmc = bc(tab[31])
    metap = np.zeros((128, D), f32)
    metap[:16] = np.asarray(meta_tokens, f32)

    k = np.arange(128)[:, None, None]
    kt = np.arange(2)[None, :, None]
    q = np.arange(128)[None, None, :]
    d = q + 128 - (kt * 128 + k)
    valid = (d >= 0) & (d < 128)
    bw = tab[_t5_bucket(d)]
    bw = np.where(valid[..., None], bw, f32(NEG)).transpose(0, 1, 3, 2)
    ebw = np.ascontiguousarray(bw.reshape(128, 2048).astype(f32))
    maskdiag = np.where(np.arange(128)[:, None] <= np.arange(128)[None, :], f32(1), f32(0)).astype(f32)
    ident = np.eye(128, dtype=f32)
    tri = np.triu(np.ones((128, 128), f32))
    ones = np.ones((128, 128), f32)
    sel63 = np.zeros((128, 128), f32)
    sel63[63, :] = 1.0

    in_maps = []
    for c in range(8):
        b, par = c // 2, c % 2
        if par == 1:
            xr = x[b]
        else:
            xr = np.concatenate([np.zeros((128, D), f32), x[b][:-128]], axis=0)
        mi = np.arange(16)[:, None]
        qq = np.arange(128)[None, :]
        dm = 16 + par * 128 + qq - mi
        bm0 = tab[_t5_bucket(dm)].transpose(0, 2, 1)
        ebm0 = np.zeros((128, 8, 128), f32)
        ebm0[:16] = bm0
        cst = np.zeros((128, 8), f32)
        cst[:, 0] = 1.0 if par == 1 else 0.0
        cst[:, 1] = 0.0 if par == 1 else NEG
        cst[:16, 2] = 1.0
        in_maps.append({
            "xr": np.ascontiguousarray(xr), "metap": metap, "w_in": w_in_r, "w_out": w_out0, "w_gu": w_gu_r,
            "w_dn": w_dn0, "g_pre_mix": g1, "g_post_mix": g2, "g_pre_ffn": g3, "g_post_ffn": g4,
            "bfb": bfb, "sinkb": sinkb, "ebw": ebw, "ebm0": np.ascontiguousarray(ebm0.reshape(128, 1024)),
            "ebmc": ebmc, "maskdiag": maskdiag, "ident": ident, "tri": tri, "ones": ones, "sel63": sel63,
            "cst": cst,
        })
    return in_maps


def kernel(x, meta_tokens, rel_bias, ln_pre_mix, ln_post_mix, ln_pre_ffn, ln_post_ffn,
           w_in, b_forget, sinks, w_out, w_gate_up, w_down):
    f32 = np.float32
    in_maps = prep(x, meta_tokens, rel_bias, ln_pre_mix, ln_post_mix, ln_pre_ffn, ln_post_ffn,
                   w_in, b_forget, sinks, w_out, w_gate_up, w_down)
    B = np.asarray(x).shape[0]
    nc = build_program()
    res = run_bass_kernel_spmd(nc, in_maps, core_ids=list(range(8)))
    outp = np.zeros((B, 4096, D), f32)
    for c in range(8):
        b, par = c // 2, c % 2
        o = np.asarray(res.results[c]["out"], f32).reshape(NS, 128, D)
        outp[b].reshape(32, 128, D)[par::2] = o
    return outp
```
